# Optimizing a Trainium2 kernel written in Bass

```python
import math
import jax, jax.numpy as jnp
from jax import lax
import numpy as np

D_MODEL = 4096
BATCH = 4
SEQ = 4096
DEPTH = 1

N_META = 16
MIX_WIDTH = D_MODEL
ATTN_WIDTH = MIX_WIDTH // 2
FOURIER_WIDTH = MIX_WIDTH - ATTN_WIDTH
DIFF_HEAD_DIM = 64
DIFF_V_DIM = 2 * DIFF_HEAD_DIM
N_DIFF_HEADS = ATTN_WIDTH // DIFF_V_DIM
QK_WIDTH = N_DIFF_HEADS * 2 * DIFF_HEAD_DIM
N_FOURIER_GROUPS = 4
FOURIER_GROUP_DIM = FOURIER_WIDTH // N_FOURIER_GROUPS
IN_WIDTH = 2 * QK_WIDTH + ATTN_WIDTH + FOURIER_WIDTH
D_FF = 256 * ((int(8 * D_MODEL / 3) + 255) // 256)
CONV_WIDTH = 3
Q_BLOCK = 128
NORM_EPS = 1e-6
SUBLN_EPS = 1e-5

kernel_name = "hymba_diffattn_fnet_convglu_encoder"


def rms_norm(x, w, eps=NORM_EPS):
    xf = x.astype(jnp.float32)
    y = xf * lax.rsqrt(jnp.mean(xf * xf, axis=-1, keepdims=True) + eps)
    return (y * w.astype(jnp.float32)).astype(x.dtype)


def alibi_slopes(n_heads):
    return jnp.exp2(-8.0 * jnp.arange(1, n_heads + 1, dtype=jnp.float32) / n_heads)


def diff_attention(q, k, v, lam):
    b, l, _ = q.shape
    q = q.reshape(b, l, N_DIFF_HEADS, 2, DIFF_HEAD_DIM).transpose(0, 2, 3, 1, 4)
    k = k.reshape(b, l, N_DIFF_HEADS, 2, DIFF_HEAD_DIM).transpose(0, 2, 3, 1, 4)
    v = v.reshape(b, l, N_DIFF_HEADS, DIFF_V_DIM).transpose(0, 2, 1, 3)
    slopes = alibi_slopes(N_DIFF_HEADS)
    k_pos = jnp.arange(l, dtype=jnp.float32)
    scale = 1.0 / math.sqrt(DIFF_HEAD_DIM)

    def attend(qb, q_pos):
        s = jnp.einsum('bhcqd,bhckd->bhcqk', qb, k).astype(jnp.float32) * scale
        bias = -slopes[:, None, None] * jnp.abs(q_pos[:, None] - k_pos[None, :])
        p = jax.nn.softmax(s + bias[None, :, None], axis=-1)
        a = p[:, :, 0] - lam * p[:, :, 1]
        return jnp.einsum('bhqk,bhkd->bhqd', a.astype(v.dtype), v)

    out_meta = attend(q[:, :, :, :N_META], k_pos[:N_META])
    n_blocks = (l - N_META) // Q_BLOCK
    q_real = q[:, :, :, N_META:].reshape(b, N_DIFF_HEADS, 2, n_blocks, Q_BLOCK, DIFF_HEAD_DIM)
    q_real = jnp.moveaxis(q_real, 3, 0)
    pos_real = k_pos[N_META:].reshape(n_blocks, Q_BLOCK)
    out_real = lax.map(lambda args: attend(args[0], args[1]), (q_real, pos_real))
    out_real = jnp.moveaxis(out_real, 0, 2).reshape(b, N_DIFF_HEADS, n_blocks * Q_BLOCK, DIFF_V_DIM)
    return jnp.concatenate([out_meta, out_real], axis=2)


def fourier_mix(f, norm_w):
    b, l, _ = f.shape
    fg = f.astype(jnp.float32).reshape(b, l, N_FOURIER_GROUPS, FOURIER_GROUP_DIM)
    fr = jnp.fft.fft2(fg, axes=(1, 3), norm="ortho").real
    fr = rms_norm(fr, norm_w.reshape(N_FOURIER_GROUPS, FOURIER_GROUP_DIM))
    return fr.reshape(b, l, FOURIER_WIDTH).astype(f.dtype)


def dwconv_centred(h, w, bias):
    hp = jnp.pad(h, ((0, 0), (1, 1), (0, 0)))
    return hp[:, :-2] * w[0] + hp[:, 1:-1] * w[1] + hp[:, 2:] * w[2] + bias


def setup_inputs(seed: int = 0) -> dict:
    key = jax.random.key(seed)
    ks = jax.random.split(key, 16)
    f32 = jnp.float32
    nrm = lambda k, shape, s: jax.random.normal(k, shape, f32) * s
    gain = lambda k, shape: 1.0 + 0.02 * jax.random.normal(k, shape, f32)
    return {
        "x": nrm(ks[0], (BATCH, SEQ, D_MODEL), 1.0),
        "meta_tokens": nrm(ks[1], (N_META, D_MODEL), 1.0),
        "norm1_w": gain(ks[2], (DEPTH, D_MODEL)),
        "w_in": nrm(ks[3], (DEPTH, D_MODEL, IN_WIDTH), D_MODEL ** -0.5),
        "lambda_q1": nrm(ks[4], (DEPTH, DIFF_HEAD_DIM), 0.1),
        "lambda_k1": nrm(ks[5], (DEPTH, DIFF_HEAD_DIM), 0.1),
        "lambda_q2": nrm(ks[6], (DEPTH, DIFF_HEAD_DIM), 0.1),
        "lambda_k2": nrm(ks[7], (DEPTH, DIFF_HEAD_DIM), 0.1),
        "diff_subln_w": gain(ks[8], (DEPTH, DIFF_V_DIM)),
        "fourier_norm_w": gain(ks[9], (DEPTH, FOURIER_WIDTH)),
        "w_out": nrm(ks[10], (DEPTH, MIX_WIDTH, D_MODEL), MIX_WIDTH ** -0.5),
        "norm2_w": gain(ks[11], (DEPTH, D_MODEL)),
        "w_up": nrm(ks[12], (DEPTH, D_MODEL, 2 * D_FF), D_MODEL ** -0.5),
        "conv_w": nrm(ks[13], (DEPTH, CONV_WIDTH, D_FF), CONV_WIDTH ** -0.5),
        "conv_b": nrm(ks[14], (DEPTH, D_FF), 0.01),
        "w_down": nrm(ks[15], (DEPTH, D_FF, D_MODEL), D_FF ** -0.5),
        "final_norm_w": gain(jax.random.fold_in(key, 99), (D_MODEL,)),
    }


def reference(x, meta_tokens, norm1_w, w_in, lambda_q1, lambda_k1, lambda_q2, lambda_k2,
              diff_subln_w, fourier_norm_w, w_out, norm2_w, w_up, conv_w, conv_b, w_down,
              final_norm_w):
    b = x.shape[0]
    meta = jnp.broadcast_to(meta_tokens[None].astype(x.dtype), (b, N_META, D_MODEL))
    h = jnp.concatenate([meta, x], axis=1)
    l = h.shape[1]
    for layer in range(DEPTH):
        lambda_init = 0.8 - 0.6 * math.exp(-0.3 * layer)
        hn = rms_norm(h, norm1_w[layer])
        y = hn @ w_in[layer]
        q = y[..., :QK_WIDTH]
        k = y[..., QK_WIDTH:2 * QK_WIDTH]
        v = y[..., 2 * QK_WIDTH:2 * QK_WIDTH + ATTN_WIDTH]
        f = y[..., 2 * QK_WIDTH + ATTN_WIDTH:]
        lam = (jnp.exp(jnp.sum(lambda_q1[layer].astype(jnp.float32) * lambda_k1[layer].astype(jnp.float32)))
               - jnp.exp(jnp.sum(lambda_q2[layer].astype(jnp.float32) * lambda_k2[layer].astype(jnp.float32)))
               + lambda_init)
        attn = diff_attention(q, k, v, lam)
        attn = rms_norm(attn, diff_subln_w[layer], SUBLN_EPS) * (1.0 - lambda_init)
        attn = attn.transpose(0, 2, 1, 3).reshape(b, l, ATTN_WIDTH)
        four = fourier_mix(f, fourier_norm_w[layer])
        h = h + jnp.concatenate([attn.astype(h.dtype), four], axis=-1) @ w_out[layer]
        hn = rms_norm(h, norm2_w[layer])
        u = hn @ w_up[layer]
        gate, val = u[..., :D_FF], u[..., D_FF:]
        gate = jax.nn.gelu(dwconv_centred(gate, conv_w[layer], conv_b[layer]), approximate=False)
        h = h + (gate * val) @ w_down[layer]
    return rms_norm(h[:, N_META:], final_norm_w)
```

```python
import math
from contextlib import ExitStack

import numpy as np
import ml_dtypes

import concourse.bass as bass
import concourse.mybir as mybir
from concourse.bass_utils import run_bass_kernel_spmd

F32 = mybir.dt.float32
BF16 = mybir.dt.bfloat16
AF = mybir.ActivationFunctionType
ALU = mybir.AluOpType
AX = mybir.AxisListType

NORM_EPS = 1e-6
SUBLN_EPS = 1e-5


class Cfg:
    def __init__(s, D=4096, SEQ=4096, B=4, NMETA=16, H=16, NG=4, GD=512, DFF=11008):
        s.D, s.SEQ, s.B, s.NMETA, s.H, s.NG, s.GD, s.DFF = D, SEQ, B, NMETA, H, NG, GD, DFF
        s.QK = H * 128
        s.ATT = H * 128
        s.FW = NG * GD
        s.IN = 2 * s.QK + s.ATT + s.FW
        s.MIX = s.ATT + s.FW
        assert s.MIX == D
        s.L = NMETA + SEQ
        s.HALF = SEQ // 2
        s.NOWN = s.HALF + 2
        s.NTOK = s.L + 1
        s.NOTH = s.NTOK - s.NOWN
        s.KD = D // 128
        s.NFF = DFF // 128
        s.GC = GD // 128
        assert D % 512 == 0 and DFF % 128 == 0 and s.IN % 256 == 0 and GD % 128 == 0
        s.tiles = []
        for base, n in ((0, s.NOWN), (s.NOWN, s.NOTH)):
            o = 0
            while o < n:
                m = min(128, n - o)
                s.tiles.append((base + o, m))
                o += m
        s.NKC = len(s.tiles)
        s.n_own_tiles = sum(1 for (t0, n) in s.tiles if t0 < s.NOWN)
        s.qchunks = split(s.NOWN, 512)
        s.QW = max(w for _, w in s.qchunks)
        s.lam_init = 0.8 - 0.6 * math.exp(-0.3 * 0)
        s.slopes = [2.0 ** (-8.0 * (h + 1) / H) for h in range(H)]


def split(n, mx):
    k = (n + mx - 1) // mx
    base, rem = divmod(n, k)
    out, o = [], 0
    for i in range(k):
        w = base + (1 if i < rem else 0)
        out.append((o, w))
        o += w
    return out


class Res:
    __slots__ = ("name", "writers", "readers", "sem")

    def __init__(self, name):
        self.name = name
        self.writers = []
        self.readers = []
        self.sem = None


class Sem:
    __slots__ = ("h", "count")

    def __init__(self, h):
        self.h = h
        self.count = 0


class Task:
    __slots__ = ("eng", "fn", "deps", "signal", "sem", "val", "is_dma", "idx")


COMPUTE = ("pe", "act", "dve", "pool")
ENGS = ("pe", "act", "dve", "pool", "sp")


class Prog:
    def __init__(self, nc, gstack):
        self.nc = nc
        self.gstack = gstack
        self.nsem = 0
        self.eng_sem = {e: self._new_sem(e) for e in COMPUTE}
        self.dma_pool = []
        self.dma_pool_used = 0
        self.tasks = {e: [] for e in ENGS}
        self.all = []
        self.barrier_events = []
        self.phase_dma_sems = []
        self.persist_sems = []

    def _new_sem(self, name):
        self.nsem += 1
        h = self.gstack.enter_context(self.nc.semaphore(f"s{self.nsem}_{name}"))
        return Sem(h)

    def dma_sem(self):
        if self.dma_pool_used < len(self.dma_pool):
            s = self.dma_pool[self.dma_pool_used]
        else:
            s = self._new_sem("dma")
            self.dma_pool.append(s)
        self.dma_pool_used += 1
        return s

    def persistent_sem(self):
        s = self._new_sem("pdma")
        self.persist_sems.append(s)
        return s

    def add(self, eng, fn, reads=(), writes=(), pwrites=(), key=None, qeng=None):
        t = Task()
        t.is_dma = eng == "dma"
        t.eng = qeng if t.is_dma else eng
        t.fn = fn
        deps = []
        for r in reads:
            deps.extend(r.writers)
        for w in writes:
            deps.extend(w.writers)
            deps.extend(w.readers)
        for w in pwrites:
            deps.extend(w.readers)
        seen = set()
        t.deps = []
        for d in deps:
            if id(d) in seen:
                continue
            seen.add(id(d))
            if (not d.is_dma) and (not t.is_dma) and d.eng == "pe" and t.eng == "pe":
                continue
            t.deps.append(d)
            d.signal = True
        t.signal = t.is_dma
        if t.is_dma:
            if key.sem is None:
                key.sem = self.dma_sem()
                self.phase_dma_sems.append(key.sem)
            t.sem = key.sem
            key.sem.count += 16
            t.val = key.sem.count
        else:
            t.sem = None
            t.val = None
        for w in writes:
            w.writers = [t]
            w.readers = []
        for w in pwrites:
            w.writers.append(t)
        for r in reads:
            r.readers.append(t)
        self.tasks[t.eng].append(t)
        self.all.append(t)
        return t

    def emit_phase(self, name):
        nc = self.nc
        for e in COMPUTE:
            if self.tasks[e]:
                self.tasks[e][-1].signal = True
        for e in COMPUTE:
            s = self.eng_sem[e]
            for t in self.tasks[e]:
                if t.is_dma:
                    continue
                if t.signal:
                    if s.count >= 30000:
                        s = self._new_sem(e)
                        self.eng_sem[e] = s
                    s.count += 1
                    t.sem = s
                    t.val = s.count
        barrier = list(self.barrier_events)

        def run(engname):
            def body(eng):
                waited = {}

                def wait(sem, val):
                    if waited.get(id(sem), 0) >= val:
                        return
                    waited[id(sem)] = val
                    eng.wait_ge(sem.h, val)

                for (sem, val) in barrier:
                    wait(sem, val)
                for t in self.tasks[engname]:
                    for d in t.deps:
                        wait(d.sem, d.val)
                    ins = t.fn(eng)
                    if t.signal:
                        ins.then_inc(t.sem.h, 16 if t.is_dma else 1)
            return body

        with nc.Block() as block:
            block.tensor(run("pe"))
            block.scalar(run("act"))
            block.vector(run("dve"))
            block.gpsimd(run("pool"))
            block.sync(run("sp"))
        ev = []
        for e in COMPUTE:
            if self.tasks[e]:
                lt = [t for t in self.tasks[e] if not t.is_dma]
                if lt:
                    ev.append((lt[-1].sem, lt[-1].val))
        for s in self.phase_dma_sems:
            ev.append((s, s.count))
        self.barrier_events = ev
        self.final_events = ev
        self.tasks = {e: [] for e in ENGS}
        self.phase_dma_sems = []
        self.dma_pool_used = 0

    def forget(self, *ress):
        for r in ress:
            r.writers = []
            r.readers = []


class Ring:
    def __init__(self, nc, stack, name, shape, dtype, n, psum=False):
        self.t = []
        self.r = []
        for i in range(n):
            if psum:
                self.t.append(stack.enter_context(nc.psum_tensor(f"{name}{i}", shape, dtype)))
            else:
                self.t.append(stack.enter_context(nc.sbuf_tensor(f"{name}{i}", shape, dtype)))
            self.r.append(Res(f"{name}{i}"))
        self.i = 0
        self.n = n

    def next(self):
        k = self.i % self.n
        self.i += 1
        r = self.r[k]
        if r.writers and not r.readers:
            raise RuntimeError(f"ring {r.name} oversubscribed: slot handed out again before its data was read")
        return self.t[k], r


def sb(nc, stack, name, shape, dtype):
    return stack.enter_context(nc.sbuf_tensor(name, shape, dtype)), Res(name)


def i_dma(out, in_):
    return lambda e: e.dma_start(out=out, in_=in_)


def i_mm(out, lhsT, rhs, start, stop):
    return lambda e: e.matmul(out=out, lhsT=lhsT, rhs=rhs, start=start, stop=stop)


def i_tr(out, in_, ident):
    return lambda e: e.transpose(out=out, in_=in_, identity=ident)


def i_act(out, in_, func, **kw):
    return lambda e: e.activation(out=out, in_=in_, func=func, **kw)


def i_ts(out, in0, s1, s2, op0, op1=None):
    if op1 is None:
        return lambda e: e.tensor_scalar(out=out, in0=in0, scalar1=s1, scalar2=None, op0=op0)
    return lambda e: e.tensor_scalar(out=out, in0=in0, scalar1=s1, scalar2=s2, op0=op0, op1=op1)


def i_stt(out, in0, scalar, in1, op0, op1):
    return lambda e: e.scalar_tensor_tensor(out=out, in0=in0, scalar=scalar, in1=in1, op0=op0, op1=op1)


def i_tt(out, in0, in1, op):
    return lambda e: e.tensor_tensor(out=out, in0=in0, in1=in1, op=op)


def i_cp(out, in_):
    return lambda e: e.tensor_copy(out=out, in_=in_)


def i_recip(out, in_):
    return lambda e: e.reciprocal(out=out, in_=in_)


def i_memset(ap, v):
    return lambda e: e.memset(ap, v)


def i_reduce(out, in_, axis, op):
    return lambda e: e.tensor_reduce(out=out, in_=in_, axis=axis, op=op)


def build_program(cfg):
    nc = bass.Bass("TRN2", target_bir_lowering=False)
    D, KD, NTOK, NOWN, HALF = cfg.D, cfg.KD, cfg.NTOK, cfg.NOWN, cfg.HALF
    H, NG, GD, GC, NFF, NKC, QW = cfg.H, cfg.NG, cfg.GD, cfg.GC, cfg.NFF, cfg.NKC, cfg.QW
    IN, ATT, FW, DFF = cfg.IN, cfg.ATT, cfg.FW, cfg.DFF
    NQC = len(cfg.qchunks)
    NB = D // 512

    def din(name, shape, dt=F32):
        return nc.dram_tensor(name, list(shape), dt, kind="ExternalInput").ap()

    def dscr(name, shape, dt):
        return nc.dram_tensor(name, list(shape), dt, kind="Internal").ap()

    xall = din("xall", [NTOK, D])
    posq_d = din("posq", [128, NOWN])
    posk_d = din("posk", [128, NKC])
    dft_d = din("dft", [NQC, NKC * 128, 2, QW], BF16)
    cdft_d = din("cdft", [2, 128, GC, GD], BF16)
    w_in = din("w_in", [D, IN])
    w_out = din("w_out", [cfg.MIX, D])
    w_up = din("w_up", [D, 2 * DFF])
    w_down = din("w_down", [DFF, D])
    n1w_d = din("n1w", [128, D])
    n2w_d = din("n2w", [128, D])
    nfw_d = din("nfw", [128, D])
    lam_d = din("lamv", [128, 4, 64])
    subw_d = din("subw", [128, 1])
    fw_d = din("fw", [128, NG * GC])
    cw_d = din("cw", [128, NFF, 3])
    cb_d = din("cb", [128, NFF])
    mask_d = din("gmask", [128, 1])
    ident_d = din("ident", [128, 128], BF16)
    y = nc.dram_tensor("y", [HALF, D], F32, kind="ExternalOutput").ap()

    winb = dscr("winb", [IN // 256, 128, KD, 256], BF16)
    woutb = dscr("woutb", [NB, 128, KD, 512], BF16)
    wupb = dscr("wupb", [NFF, 128, KD, 256], BF16)
    wdnb = dscr("wdnb", [NB, 128, NFF, 512], BF16)
    hnT_s = dscr("hnT_s", [D, NTOK], BF16)
    KT_s = dscr("KT_s", [H, 128, NTOK], BF16)
    QT_s = dscr("QT_s", [H, 128, NOWN], BF16)
    V_s = dscr("V_s", [NKC * 128, ATT], BF16)
    G_s = dscr("G_s", [2, NKC * 128, FW], BF16)
    mixT_s = dscr("mixT_s", [cfg.MIX, NOWN], BF16)
    h2_s = dscr("h2_s", [NOWN, D], F32)
    hn2T_s = dscr("hn2T_s", [D, NOWN], BF16)

    def kp(ap):
        return ap.rearrange("(k p) t -> p k t", p=128)

    gstack = ExitStack()
    with gstack:
        P = Prog(nc, gstack)

        def DMA(out, in_, reads=(), writes=(), pwrites=(), key=None, q="sp"):
            return P.add("dma", i_dma(out, in_), reads=reads, writes=writes, pwrites=pwrites, key=key, qeng=q)

        def acc_rw(r, first):
            return dict(writes=[r], pwrites=[]) if first else dict(writes=[], pwrites=[r])

        nqb = cfg.QK // 256
        nvb = ATT // 256
        cb_order = (list(range(nqb, 2 * nqb)) + list(range(2 * nqb, 2 * nqb + nvb))
                    + list(range(2 * nqb + nvb, IN // 256)) + list(range(0, nqb)))
        KG = min(8, KD)
        win_res = {}
        wout_res = []
        wup_res = [None] * NFF
        wdn_res = []

        def prep_win():
            for gi in range(0, len(cb_order), 2):
                r = Res(f"winb{gi}")
                r.sem = P.persistent_sem()
                for cb in cb_order[gi:gi + 2]:
                    win_res[cb] = r
                    for kg in range(0, KD, KG):
                        DMA(winb[cb, :, kg:kg + KG, :], kp(w_in[kg * 128:(kg + KG) * 128, cb * 256:(cb + 1) * 256]),
                            pwrites=[r], key=r, q="pool")

        def prep_wout():
            for nb in range(NB):
                r = Res(f"woutb{nb}")
                r.sem = P.persistent_sem()
                wout_res.append(r)
                for kg in range(0, KD, KG):
                    DMA(woutb[nb, :, kg:kg + KG, :], kp(w_out[kg * 128:(kg + KG) * 128, nb * 512:(nb + 1) * 512]),
                        pwrites=[r], key=r, q="pool")

        def prep_wup():
            JG = 16
            for j0 in range(0, NFF, JG):
                r = Res(f"wupb{j0}")
                r.sem = P.persistent_sem()
                for j in range(j0, min(NFF, j0 + JG)):
                    wup_res[j] = r
                    for kg in range(0, KD, KG):
                        for half, cbase in ((0, 0), (1, DFF)):
                            DMA(wupb[j, :, kg:kg + KG, half * 128:(half + 1) * 128],
                                kp(w_up[kg * 128:(kg + KG) * 128, cbase + j * 128:cbase + (j + 1) * 128]),
                                pwrites=[r], key=r, q="pool")

        def prep_wdn():
            FG = 8
            for nb in range(NB):
                r = Res(f"wdnb{nb}")
                r.sem = P.persistent_sem()
                wdn_res.append(r)
                for f0 in range(0, NFF, FG):
                    f1 = min(NFF, f0 + FG)
                    DMA(wdnb[nb, :, f0:f1, :], kp(w_down[f0 * 128:f1 * 128, nb * 512:(nb + 1) * 512]),
                        pwrites=[r], key=r, q="pool")

        def pipeline(n, load, compute, pf):
            for i in range(min(pf, n)):
                load(i)
            for i in range(n):
                if i + pf < n:
                    load(i + pf)
                compute(i)

        prep_win()

        hnT_res = Res("hnT_s")
        KT_res = [Res(f"KT{h}") for h in range(H)]
        QT_res = [Res(f"QT{h}") for h in range(H)]
        V_res = Res("V_s")
        G_res = [Res(f"G{g}") for g in range(NG)]
        mixA_res = [Res(f"mixA{h}") for h in range(H)]
        mixF_res = [Res(f"mixF{g}") for g in range(NG)]
        h2_res = Res("h2_s")
        hn2T_res = Res("hn2T_s")
        y_res = Res("y")

        evac_ctr = [0]

        def evac(out_ap, in_ap, reads, writes=(), pwrites=()):
            evac_ctr[0] += 1
            if evac_ctr[0] % 2:
                P.add("act", i_act(out_ap, in_ap, AF.Copy), reads=reads, writes=writes, pwrites=pwrites)
            else:
                P.add("dve", i_cp(out_ap, in_ap), reads=reads, writes=writes, pwrites=pwrites)

        def norm_phase(name, src, rows_tiles, wb_d, eps, mode, dstT=None, dst_res=None, src_res=None,
                       dst_rows=None, extra=None):
            with ExitStack() as st:
                wb, wb_r = sb(nc, st, name + "wb", [128, D], F32)
                xt = Ring(nc, st, name + "x", [128, D], F32, 3)
                ot = Ring(nc, st, name + "o", [128, D], BF16 if mode == "T" else F32, 2)
                ss = Ring(nc, st, name + "ss", [128, 1], F32, 4)
                rs = Ring(nc, st, name + "rs", [128, 1], F32, 4)
                ln = Ring(nc, st, name + "ln", [128, 1], F32, 4)
                epsc, epsc_r = sb(nc, st, name + "eps", [128, 1], F32)
                P.add("dve", i_memset(epsc[:], eps), writes=[epsc_r])
                DMA(wb[:], wb_d[:, :], writes=[wb_r], key=wb_r)
                if mode == "T":
                    ident, ident_r = sb(nc, st, name + "id", [128, 128], BF16)
                    tT = Ring(nc, st, name + "tT", [128, KD, 528], BF16, 2)
                    pst = Ring(nc, st, name + "ps", [128, 8, 128], BF16, 8, psum=True)
                    DMA(ident[:], ident_d[:, :], writes=[ident_r], key=ident_r)
                if extra is not None:
                    extra()
                groups, cur = [], []
                for (r0, n) in rows_tiles:
                    if cur and (cur[-1][0] + cur[-1][1] != r0 or (r0 + n - cur[0][0]) > 528):
                        groups.append(cur)
                        cur = []
                    cur.append((r0, n))
                if cur:
                    groups.append(cur)
                inv_sqrt_d = 1.0 / math.sqrt(D)
                sread = [src_res] if src_res is not None else []
                items = []
                for gi, grp in enumerate(groups):
                    for ti, (r0, n) in enumerate(grp):
                        items.append((r0, n, gi, ti == 0, ti == len(grp) - 1))
                xs = {}
                state = {}

                def load(i):
                    r0, n = items[i][0], items[i][1]
                    x_t, x_r = xt.next()
                    xs[i] = (x_t, x_r)
                    DMA(x_t[0:n, :], src[r0:r0 + n, :], reads=sread, writes=[x_r], key=x_r)

                def compute(i):
                    r0, n, gi, gfirst, glast = items[i]
                    grp = groups[gi]
                    g0 = grp[0][0]
                    x_t, x_r = xs.pop(i)
                    o_t, o_r = ot.next()
                    s_t, s_r = ss.next()
                    r_t, r_r = rs.next()
                    l_t, l_r = ln.next()
                    if mode == "T" and gfirst:
                        state["tT"] = tT.next()
                    P.add("act", i_act(o_t[0:n, :], x_t[0:n, :], AF.Square, scale=inv_sqrt_d,
                                       accum_out=s_t[0:n, :]), reads=[x_r], writes=[o_r, s_r])
                    P.add("act", i_act(l_t[0:n, :], s_t[0:n, :], AF.Ln, bias=epsc[0:n, 0:1]), reads=[s_r, epsc_r],
                          writes=[l_r])
                    P.add("act", i_act(r_t[0:n, :], l_t[0:n, :], AF.Exp, scale=-0.5), reads=[l_r], writes=[r_r])
                    P.add("dve", i_stt(o_t[0:n, :], x_t[0:n, :], r_t[0:n, 0:1], wb[0:n, :], ALU.mult, ALU.mult),
                          reads=[x_r, r_r, wb_r], writes=[o_r])
                    if mode == "F":
                        DMA(dst_rows[r0:r0 + n, :], o_t[0:n, :], reads=[o_r], pwrites=[dst_res], key=o_r)
                        return
                    tTt, tTr = state["tT"]
                    c0 = r0 - g0
                    for kb in range(0, KD, 8):
                        kn = min(8, KD - kb)
                        p_t, p_r = pst.next()
                        for kk in range(kn):
                            k = kb + kk
                            P.add("pe", i_tr(p_t[:, kk, 0:n], o_t[0:n, k * 128:(k + 1) * 128], ident[0:n, 0:n]),
                                  reads=[o_r, ident_r], **acc_rw(p_r, kk == 0))
                        evac(tTt[:, kb:kb + kn, c0:c0 + n], p_t[:, 0:kn, 0:n], reads=[p_r], pwrites=[tTr])
                    if glast:
                        gn = grp[-1][0] + grp[-1][1] - g0
                        DMA(kp(dstT[:, g0:g0 + gn]), tTt[:, :, 0:gn], reads=[tTr], pwrites=[dst_res], key=tTr)

                pipeline(len(items), load, compute, 2)
                P.emit_phase(name)

        norm_phase("n1", xall, cfg.tiles, n1w_d, NORM_EPS, "T", dstT=hnT_s, dst_res=hnT_res)

        supers, cur = [], []
        for (t0, n) in cfg.tiles:
            own = t0 < NOWN
            if cur and ((cur[0][0] < NOWN) != own or (t0 + n - cur[0][0]) > 1040):
                supers.append(cur)
                cur = []
            cur.append((t0, n))
        if cur:
            supers.append(cur)
        TS = max(s[-1][0] + s[-1][1] - s[0][0] for s in supers)
        tile_index = {t0: i for i, (t0, n) in enumerate(cfg.tiles)}
        with ExitStack() as st:
            hnTr = Ring(nc, st, "pj_hnT", [128, KD, TS], BF16, 2)
            wblk = Ring(nc, st, "pj_w", [128, KD, 256], BF16, 3)
            cdft, cdft_r = sb(nc, st, "pj_cdft", [128, 2, GC, GD], BF16)
            fT, fT_r = sb(nc, st, "pj_fT", [128, GC, TS], BF16)
            stg = Ring(nc, st, "pj_stg", [128, 512], BF16, 4)
            acc = Ring(nc, st, "pj_acc", [128, 512], F32, 8, psum=True)
            prep_wout()
            prep_wup()
            for cs in range(2):
                DMA(cdft[:, cs, :, :], cdft_d[cs], pwrites=[cdft_r], key=cdft_r)
            items = []
            for si, stiles in enumerate(supers):
                own = stiles[0][0] < NOWN
                blocks = [c for c in cb_order if own or c >= nqb]
                for bi, cb in enumerate(blocks):
                    items.append((si, cb, bi == 0))
            ws = {}
            hs = {}

            def pj_load(i):
                si, cb, sfirst = items[i]
                if sfirst:
                    stiles = supers[si]
                    s0 = stiles[0][0]
                    sn = stiles[-1][0] + stiles[-1][1] - s0
                    h_t, h_r = hnTr.next()
                    hs[si] = (h_t, h_r)
                    DMA(h_t[:, :, 0:sn], kp(hnT_s[:, s0:s0 + sn]), reads=[hnT_res], writes=[h_r], key=h_r)
                w_t, w_r = wblk.next()
                ws[i] = (w_t, w_r)
                DMA(w_t[:], winb[cb], reads=[win_res[cb]], writes=[w_r], key=w_r)

            def pj_compute(i):
                si, cb, sfirst = items[i]
                stiles = supers[si]
                s0 = stiles[0][0]
                sn = stiles[-1][0] + stiles[-1][1] - s0
                mch = split(sn, 512)
                hnT, hnT_r = hs[si]
                w_t, w_r = ws.pop(i)
                col0 = cb * 256
                if col0 < 2 * cfg.QK:
                    isq = col0 < cfg.QK
                    for hh in range(2):
                        h = (col0 - (0 if isq else cfg.QK)) // 128 + hh
                        for (m0, mw) in mch:
                            a_t, a_r = acc.next()
                            for k in range(KD):
                                P.add("pe", i_mm(a_t[:, 0:mw], w_t[:, k, hh * 128:(hh + 1) * 128],
                                                 hnT[:, k, m0:m0 + mw], k == 0, k == KD - 1),
                                      reads=[w_r, hnT_r], **acc_rw(a_r, k == 0))
                            g_t, g_r = stg.next()
                            evac(g_t[:, 0:mw], a_t[:, 0:mw], reads=[a_r], writes=[g_r])
                            if isq:
                                dst, dres = QT_s[h, :, s0 + m0:s0 + m0 + mw], QT_res[h]
                            else:
                                dst, dres = KT_s[h, :, s0 + m0:s0 + m0 + mw], KT_res[h]
                            DMA(dst, g_t[:, 0:mw], reads=[g_r], pwrites=[dres], key=g_r)
                elif col0 < 2 * cfg.QK + ATT:
                    vc0 = col0 - 2 * cfg.QK
                    for (t0, n) in stiles:
                        a_t, a_r = acc.next()
                        for k in range(KD):
                            P.add("pe", i_mm(a_t[0:n, 0:256], hnT[:, k, t0 - s0:t0 - s0 + n], w_t[:, k, :],
                                             k == 0, k == KD - 1), reads=[w_r, hnT_r], **acc_rw(a_r, k == 0))
                        g_t, g_r = stg.next()
                        evac(g_t[0:n, 0:256], a_t[0:n, 0:256], reads=[a_r], writes=[g_r])
                        ti = tile_index[t0]
                        DMA(V_s[ti * 128:ti * 128 + n, vc0:vc0 + 256], g_t[0:n, 0:256], reads=[g_r],
                            pwrites=[V_res], key=g_r)
                else:
                    fc0 = col0 - 2 * cfg.QK - ATT
                    g = fc0 // GD
                    cc0 = (fc0 % GD) // 128
                    for hh in range(2):
                        for (m0, mw) in mch:
                            a_t, a_r = acc.next()
                            for k in range(KD):
                                P.add("pe", i_mm(a_t[:, 0:mw], w_t[:, k, hh * 128:(hh + 1) * 128],
                                                 hnT[:, k, m0:m0 + mw], k == 0, k == KD - 1),
                                      reads=[w_r, hnT_r], **acc_rw(a_r, k == 0))
                            evac(fT[:, cc0 + hh, m0:m0 + mw], a_t[:, 0:mw], reads=[a_r], pwrites=[fT_r])
                    if cc0 + 2 >= GC:
                        for (t0, n) in stiles:
                            ti = tile_index[t0]
                            for cs in range(2):
                                a_t, a_r = acc.next()
                                for c in range(GC):
                                    P.add("pe", i_mm(a_t[0:n, 0:GD], fT[:, c, t0 - s0:t0 - s0 + n],
                                                     cdft[:, cs, c, :], c == 0, c == GC - 1),
                                          reads=[fT_r, cdft_r], **acc_rw(a_r, c == 0))
                                g_t, g_r = stg.next()
                                evac(g_t[0:n, 0:GD], a_t[0:n, 0:GD], reads=[a_r], writes=[g_r])
                                DMA(G_s[cs, ti * 128:ti * 128 + n, g * GD:(g + 1) * GD], g_t[0:n, 0:GD],
                                    reads=[g_r], pwrites=[G_res[g]], key=g_r)

            pipeline(len(items), pj_load, pj_compute, 2)
            P.emit_phase("proj")

        with ExitStack() as st:
            posq, posq_r = sb(nc, st, "at_posq", [128, NOWN], F32)
            posk, posk_r = sb(nc, st, "at_posk", [128, NKC], F32)
            lamv, lamv_r = sb(nc, st, "at_lamv", [128, 4, 64], F32)
            lprod, lprod_r = sb(nc, st, "at_lprod", [128, 2, 64], F32)
            lsum, lsum_r = sb(nc, st, "at_lsum", [128, 2], F32)
            lexp, lexp_r = sb(nc, st, "at_lexp", [128, 2], F32)
            nlam, nlam_r = sb(nc, st, "at_nlam", [128, 1], F32)
            subw0, subw0_r = sb(nc, st, "at_subw0", [128, 1], F32)
            subw, subw_r = sb(nc, st, "at_subw", [128, 1], F32)
            onesb, onesb_r = sb(nc, st, "at_ones", [128, 128], BF16)
            Dt = st.enter_context(nc.sbuf_tensor("at_D", [128, NKC, QW], F32))
            D_r = [Res(f"D{j}") for j in range(NKC)]
            KT = Ring(nc, st, "at_KT", [128, NTOK], BF16, 2)
            Vt = Ring(nc, st, "at_V", [128, NKC, 128], BF16, 2)
            QT = Ring(nc, st, "at_QT", [128, QW], BF16, 2)
            tmp = Ring(nc, st, "at_tmp", [128, 2, QW], F32, 3)
            Et = Ring(nc, st, "at_E", [128, 2, QW], BF16, 4)
            Sp = Ring(nc, st, "at_S", [128, 2, 512], F32, 2, psum=True)
            accs = [st.enter_context(nc.psum_tensor(f"at_acc{i}", [128, 512], F32)) for i in range(4)]
            accr = [Res(f"at_acc{i}") for i in range(4)]
            rden = Ring(nc, st, "at_rden", [128, 2, QW], F32, 2)
            a12 = Ring(nc, st, "at_a12", [128, 2, QW], F32, 2)
            attn = Ring(nc, st, "at_attn", [128, QW], F32, 2)
            sq = Ring(nc, st, "at_sq", [128, QW], BF16, 2)
            rstd = Ring(nc, st, "at_rstd", [128, QW], F32, 2)
            ostg = Ring(nc, st, "at_o", [128, QW], BF16, 3)
            lnr = Ring(nc, st, "at_ln", [128, QW], F32, 2)
            epsc, epsc_r = sb(nc, st, "at_eps", [128, 1], F32)
            nposk, nposk_r = sb(nc, st, "at_nposk", [128, NKC], F32)
            prep_wdn()
            P.add("dve", i_memset(epsc[:], SUBLN_EPS), writes=[epsc_r])

            DMA(posq[:], posq_d[:, :], writes=[posq_r], key=posq_r)
            DMA(posk[:], posk_d[:, :], writes=[posk_r], key=posk_r)
            P.add("dve", i_ts(nposk[:, :], posk[:, :], -1.0, None, ALU.mult), reads=[posk_r], writes=[nposk_r])
            DMA(lamv[:], lam_d[:, :, :], writes=[lamv_r], key=lamv_r)
            DMA(subw0[:], subw_d[:, :], writes=[subw0_r], key=subw0_r)
            P.add("pool", i_memset(onesb[:], 1.0), writes=[onesb_r])
            P.add("dve", i_tt(lprod[:, 0, :], lamv[:, 0, :], lamv[:, 1, :], ALU.mult), reads=[lamv_r], writes=[lprod_r])
            P.add("dve", i_tt(lprod[:, 1, :], lamv[:, 2, :], lamv[:, 3, :], ALU.mult), reads=[lamv_r], pwrites=[lprod_r])
            P.add("dve", i_reduce(lsum[:, :], lprod[:, :, :], AX.X, ALU.add), reads=[lprod_r], writes=[lsum_r])
            P.add("act", i_act(lexp[:, :], lsum[:, :], AF.Exp), reads=[lsum_r], writes=[lexp_r])
            P.add("dve", i_stt(nlam[:, :], lexp[:, 1:2], -cfg.lam_init, lexp[:, 0:1], ALU.add, ALU.subtract),
                  reads=[lexp_r], writes=[nlam_r])
            P.add("dve", i_ts(subw[:, :], subw0[:, :], (1.0 - cfg.lam_init), None, ALU.mult),
                  reads=[subw0_r], writes=[subw_r])

            inv_sqrt_dv = 1.0 / math.sqrt(128.0)
            items = [(qi, h) for qi in range(NQC) for h in range(H)]
            ld = {}

            def at_load(i):
                qi, h = items[i]
                q0, w = cfg.qchunks[qi]
                k_t, k_r = KT.next()
                v_t, v_r = Vt.next()
                q_t, q_r = QT.next()
                ld[i] = (k_t, k_r, v_t, v_r, q_t, q_r)
                DMA(k_t[:], KT_s[h], reads=[KT_res[h]], writes=[k_r], key=k_r)
                DMA(v_t[:], V_s[:, h * 128:(h + 1) * 128].rearrange("(j p) c -> p j c", p=128),
                    reads=[V_res], writes=[v_r], key=v_r)
                DMA(q_t[:, 0:w], QT_s[h, :, q0:q0 + w], reads=[QT_res[h]], writes=[q_r], key=q_r)

            def at_compute(i):
                qi, h = items[i]
                q0, w = cfg.qchunks[qi]
                k_t, k_r, v_t, v_r, q_t, q_r = ld.pop(i)
                cneg = -8.0 * cfg.slopes[h]
                if h == 0:
                    for j, (t0, nj) in enumerate(cfg.tiles):
                        P.add("act", i_act(Dt[0:nj, j, 0:w], posq[0:nj, q0:q0 + w], AF.Abs, bias=nposk[0:nj, j:j + 1]),
                              reads=[posq_r, nposk_r], writes=[D_r[j]])

                def qk(j):
                    t0, nj = cfg.tiles[j]
                    s_t, s_r = Sp.next()
                    for c in range(2):
                        P.add("pe", i_mm(s_t[0:nj, c, 0:w], k_t[c * 64:(c + 1) * 64, t0:t0 + nj],
                                         q_t[c * 64:(c + 1) * 64, 0:w], True, True),
                              reads=[k_r, q_r], **acc_rw(s_r, c == 0))
                    return s_t, s_r

                def pv(e_t, e_r, j, nj):
                    for c in range(2):
                        P.add("pe", i_mm(accs[c][:, 0:w], v_t[0:nj, j, :], e_t[0:nj, c, 0:w], j == 0, j == NKC - 1),
                              reads=[e_r, v_r], **acc_rw(accr[c], j == 0))
                        P.add("pe", i_mm(accs[2 + c][:, 0:w], onesb[0:nj, :], e_t[0:nj, c, 0:w], j == 0, j == NKC - 1),
                              reads=[e_r, onesb_r], **acc_rw(accr[2 + c], j == 0))

                LAG = 2
                sq_ = {}
                for j in range(min(LAG, NKC)):
                    sq_[j] = qk(j)
                for j, (t0, nj) in enumerate(cfg.tiles):
                    s_t, s_r = sq_.pop(j)
                    m_t, m_r = tmp.next()
                    e_t, e_r = Et.next()
                    for c in range(2):
                        P.add("dve", i_stt(m_t[0:nj, c, 0:w], Dt[0:nj, j, 0:w], cneg, s_t[0:nj, c, 0:w],
                                           ALU.mult, ALU.add), reads=[D_r[j], s_r], **acc_rw(m_r, c == 0))
                    P.add("act", i_act(e_t[0:nj, :, 0:w], m_t[0:nj, :, 0:w], AF.Exp, scale=0.125),
                          reads=[m_r], writes=[e_r])
                    if j + LAG < NKC:
                        sq_[j + LAG] = qk(j + LAG)
                    pv(e_t, e_r, j, nj)
                rd_t, rd_r = rden.next()
                a_t, a_r = a12.next()
                at_t, at_r = attn.next()
                sq_t, sq_r = sq.next()
                rs_t, rs_r = rstd.next()
                o_t, o_r = ostg.next()
                for c in range(2):
                    P.add("dve", i_recip(rd_t[:, c, 0:w], accs[2 + c][:, 0:w]), reads=[accr[2 + c]], **acc_rw(rd_r, c == 0))
                for c in range(2):
                    P.add("dve", i_tt(a_t[:, c, 0:w], accs[c][:, 0:w], rd_t[:, c, 0:w], ALU.mult),
                          reads=[accr[c], rd_r], **acc_rw(a_r, c == 0))
                P.add("dve", i_stt(at_t[:, 0:w], a_t[:, 1, 0:w], nlam[:, 0:1], a_t[:, 0, 0:w], ALU.mult, ALU.add),
                      reads=[a_r, nlam_r], writes=[at_r])
                P.add("act", i_act(sq_t[:, 0:w], at_t[:, 0:w], AF.Square, scale=inv_sqrt_dv),
                      reads=[at_r], writes=[sq_r])
                P.add("pe", i_mm(accs[2][:, 0:w], onesb[:, :], sq_t[:, 0:w], True, True),
                      reads=[sq_r, onesb_r], writes=[accr[2]])
                ln_t, ln_r = lnr.next()
                P.add("act", i_act(ln_t[:, 0:w], accs[2][:, 0:w], AF.Ln, bias=epsc[:, 0:1]), reads=[accr[2], epsc_r],
                      writes=[ln_r])
                P.add("act", i_act(rs_t[:, 0:w], ln_t[:, 0:w], AF.Exp, scale=-0.5), reads=[ln_r], writes=[rs_r])
                P.add("dve", i_stt(o_t[:, 0:w], at_t[:, 0:w], subw[:, 0:1], rs_t[:, 0:w], ALU.mult, ALU.mult),
                      reads=[at_r, rs_r, subw_r], writes=[o_r])
                DMA(mixT_s[h * 128:(h + 1) * 128, q0:q0 + w], o_t[:, 0:w], reads=[o_r], pwrites=[mixA_res[h]],
                    key=o_r)

            pipeline(len(items), at_load, at_compute, 1)
            P.emit_phase("att")

        JB = 4
        nrm = 1.0 / math.sqrt(cfg.L * GD)
        with ExitStack() as st:
            G1r = Ring(nc, st, "fo_G1", [128, NKC, GD], BF16, 2)
            G2r = Ring(nc, st, "fo_G2", [128, NKC, GD], BF16, 2)
            slab = Ring(nc, st, "fo_slab", [128, JB, 2, QW], BF16, 4)
            fw0, fw0_r = sb(nc, st, "fo_fw0", [128, NG * GC], F32)
            fw, fw_r = sb(nc, st, "fo_fw", [128, NG * GC], F32)
            onesb, onesb_r = sb(nc, st, "fo_ones", [128, 128], BF16)
            acc = Ring(nc, st, "fo_acc", [128, 512], F32, 8, psum=True)
            sq = Ring(nc, st, "fo_sq", [128, QW], BF16, 4)
            rstd = Ring(nc, st, "fo_rstd", [128, QW], F32, 2)
            ostg = Ring(nc, st, "fo_o", [128, QW], BF16, 4)
            lnr = Ring(nc, st, "fo_ln", [128, QW], F32, 2)
            epsc, epsc_r = sb(nc, st, "fo_eps", [128, 1], F32)
            P.add("dve", i_memset(epsc[:], NORM_EPS), writes=[epsc_r])
            DMA(fw0[:], fw_d[:, :], writes=[fw0_r], key=fw0_r)
            P.add("pool", i_memset(onesb[:], 1.0), writes=[onesb_r])
            P.add("dve", i_ts(fw[:, :], fw0[:, :], nrm, None, ALU.mult), reads=[fw0_r], writes=[fw_r])
            sq_scale = nrm / math.sqrt(GD)
            slabs = list(range(0, NKC, JB))
            items = [(g, qi, si) for g in range(NG) for qi in range(NQC) for si in range(len(slabs))]
            gl = {}
            sl = {}
            fstate = {}

            def fo_load_g(g):
                g1 = G1r.next()
                g2 = G2r.next()
                gl[g] = (g1, g2)
                for cs, (gt, gr) in enumerate((g1, g2)):
                    DMA(gt[:], G_s[cs, :, g * GD:(g + 1) * GD].rearrange("(j p) c -> p j c", p=128),
                        reads=[G_res[g]], writes=[gr], key=gr)

            def fo_load(i):
                g, qi, si = items[i]
                j0 = slabs[si]
                jn = min(JB, NKC - j0)
                sl_t, sl_r = slab.next()
                sl[i] = (sl_t, sl_r)
                DMA(sl_t[:, 0:jn, :, :],
                    dft_d[qi, j0 * 128:(j0 + jn) * 128, :, :].rearrange("(j p) s q -> p j s q", p=128),
                    writes=[sl_r], key=sl_r)

            def fo_compute(i):
                g, qi, si = items[i]
                q0, w = cfg.qchunks[qi]
                j0 = slabs[si]
                jn = min(JB, NKC - j0)
                sl_t, sl_r = sl.pop(i)
                if qi == 0 and si == 0 and g + 1 < NG:
                    fo_load_g(g + 1)
                if si == 0:
                    fstate["a"] = [acc.next() for _ in range(GC)]
                a = fstate["a"]
                gpair = gl[g]
                for jj in range(jn):
                    j = j0 + jj
                    nj = cfg.tiles[j][1]
                    for cs in range(2):
                        gt, gr = gpair[cs]
                        first = (j == 0 and cs == 0)
                        lastm = (j == NKC - 1 and cs == 1)
                        for cc in range(GC):
                            P.add("pe", i_mm(a[cc][0][:, 0:w], gt[0:nj, j, cc * 128:(cc + 1) * 128],
                                             sl_t[0:nj, jj, cs, 0:w], first, lastm),
                                  reads=[gr, sl_r], **acc_rw(a[cc][1], first))
                if si != len(slabs) - 1:
                    return
                ss_t, ss_r = acc.next()
                sqs = []
                for cc in range(GC):
                    sq_t, sq_r = sq.next()
                    sqs.append((sq_t, sq_r))
                    P.add("act", i_act(sq_t[:, 0:w], a[cc][0][:, 0:w], AF.Square, scale=sq_scale),
                          reads=[a[cc][1]], writes=[sq_r])
                for cc in range(GC):
                    P.add("pe", i_mm(ss_t[:, 0:w], onesb[:, :], sqs[cc][0][:, 0:w], cc == 0, cc == GC - 1),
                          reads=[sqs[cc][1], onesb_r], **acc_rw(ss_r, cc == 0))
                rs_t, rs_r = rstd.next()
                ln_t, ln_r = lnr.next()
                P.add("act", i_act(ln_t[:, 0:w], ss_t[:, 0:w], AF.Ln, bias=epsc[:, 0:1]), reads=[ss_r, epsc_r],
                      writes=[ln_r])
                P.add("act", i_act(rs_t[:, 0:w], ln_t[:, 0:w], AF.Exp, scale=-0.5), reads=[ln_r], writes=[rs_r])
                for cc in range(GC):
                    o_t, o_r = ostg.next()
                    col = g * GC + cc
                    P.add("dve", i_stt(o_t[:, 0:w], a[cc][0][:, 0:w], fw[:, col:col + 1], rs_t[:, 0:w],
                                       ALU.mult, ALU.mult), reads=[a[cc][1], rs_r, fw_r], writes=[o_r])
                    r0 = ATT + g * GD + cc * 128
                    DMA(mixT_s[r0:r0 + 128, q0:q0 + w], o_t[:, 0:w], reads=[o_r], pwrites=[mixF_res[g]], key=o_r)

            fo_load_g(0)
            pipeline(len(items), fo_load, fo_compute, 3)
            P.emit_phase("four")

        own_tiles = cfg.tiles[:cfg.n_own_tiles]
        wgroups, cur = [], []
        for (t0, n) in own_tiles:
            if cur and (t0 + n - cur[0][0]) > 528:
                wgroups.append(cur)
                cur = []
            cur.append((t0, n))
        if cur:
            wgroups.append(cur)
        mix_all = mixA_res + mixF_res
        with ExitStack() as st:
            mixT = Ring(nc, st, "wo_mix", [128, KD, 528], BF16, 2)
            wblk = Ring(nc, st, "wo_w", [128, KD, 512], BF16, 2)
            xp = Ring(nc, st, "wo_x", [128, 512], F32, 12)
            hp = Ring(nc, st, "wo_h", [128, 512], F32, 4)
            acc = Ring(nc, st, "wo_acc", [128, 512], F32, 8, psum=True)
            items = [(gi, nb) for gi in range(len(wgroups)) for nb in range(NB)]
            ms = {}
            ws = {}
            xs = {}

            def wo_load_mix(gi):
                grp = wgroups[gi]
                g0 = grp[0][0]
                gn = grp[-1][0] + grp[-1][1] - g0
                m_t, m_r = mixT.next()
                ms[gi] = (m_t, m_r)
                DMA(m_t[:, :, 0:gn], kp(mixT_s[:, g0:g0 + gn]), reads=mix_all, writes=[m_r], key=m_r)

            def wo_load(i):
                gi, nb = items[i]
                w_t, w_r = wblk.next()
                ws[i] = (w_t, w_r)
                DMA(w_t[:], woutb[nb], reads=[wout_res[nb]], writes=[w_r], key=w_r)
                xl = []
                for (t0, n) in wgroups[gi]:
                    x_t, x_r = xp.next()
                    xl.append((x_t, x_r))
                    DMA(x_t[0:n, :], xall[t0:t0 + n, nb * 512:(nb + 1) * 512], writes=[x_r], key=x_r)
                xs[i] = xl

            def wo_compute(i):
                gi, nb = items[i]
                grp = wgroups[gi]
                g0 = grp[0][0]
                if nb == 0 and gi + 1 < len(wgroups):
                    wo_load_mix(gi + 1)
                m_t, m_r = ms[gi]
                w_t, w_r = ws.pop(i)
                xl = xs.pop(i)
                for ti, (t0, n) in enumerate(grp):
                    a_t, a_r = acc.next()
                    x_t, x_r = xl[ti]
                    h_t, h_r = hp.next()
                    for k in range(KD):
                        P.add("pe", i_mm(a_t[0:n, :], m_t[:, k, t0 - g0:t0 - g0 + n], w_t[:, k, :], k == 0, k == KD - 1),
                              reads=[m_r, w_r], **acc_rw(a_r, k == 0))
                    P.add("dve", i_tt(h_t[0:n, :], a_t[0:n, :], x_t[0:n, :], ALU.add), reads=[a_r, x_r], writes=[h_r])
                    DMA(h2_s[t0:t0 + n, nb * 512:(nb + 1) * 512], h_t[0:n, :], reads=[h_r], pwrites=[h2_res], key=h_r)

            wo_load_mix(0)
            pipeline(len(items), wo_load, wo_compute, 1)
            P.emit_phase("wout")

        norm_phase("n2", h2_s, own_tiles, n2w_d, NORM_EPS, "T", dstT=hn2T_s, dst_res=hn2T_res, src_res=h2_res)

        fgroups = split(HALF, 512)
        TMAX = max(w for _, w in fgroups)
        KS = 4
        with ExitStack() as st:
            hn2T, hn2T_r = sb(nc, st, "ff_hn", [128, KD, TMAX + 2], BF16)
            actT = st.enter_context(nc.sbuf_tensor("ff_act", [128, NFF, TMAX], BF16))
            act_r = Res("ff_act")
            wup = Ring(nc, st, "ff_wup", [128, KD, 256], BF16, 2)
            wdn = Ring(nc, st, "ff_wdn", [128, KS, 512], BF16, 4)
            cw, cw_r = sb(nc, st, "ff_cw", [128, NFF, 3], F32)
            cbv, cb_r = sb(nc, st, "ff_cb", [128, NFF], F32)
            gm, gm_r = sb(nc, st, "ff_gm", [128, 1], F32)
            gs = Ring(nc, st, "ff_gs", [128, TMAX + 2], F32, 2)
            gh = Ring(nc, st, "ff_gh", [128, 1], F32, 2)
            cA = Ring(nc, st, "ff_cA", [128, TMAX], F32, 2)
            cB = Ring(nc, st, "ff_cB", [128, TMAX], F32, 2)
            hp = Ring(nc, st, "ff_hp", [128, 512], F32, 6)
            op = Ring(nc, st, "ff_op", [128, 512], F32, 3)
            acc = Ring(nc, st, "ff_acc", [128, 512], F32, 8, psum=True)
            DMA(cw[:], cw_d[:, :, :], writes=[cw_r], key=cw_r)
            DMA(cbv[:], cb_d[:, :], writes=[cb_r], key=cb_r)
            DMA(gm[:], mask_d[:, :], writes=[gm_r], key=gm_r)
            for gi, (r0, T) in enumerate(fgroups):
                lastg = gi == len(fgroups) - 1
                DMA(hn2T[:, :, 0:T + 2], kp(hn2T_s[:, r0:r0 + T + 2]), reads=[hn2T_res], writes=[hn2T_r], key=hn2T_r)
                gch = split(T + 2, 512)
                ws = {}

                def up_load(j):
                    w_t, w_r = wup.next()
                    ws[j] = (w_t, w_r)
                    DMA(w_t[:], wupb[j], reads=[wup_res[j]], writes=[w_r], key=w_r)

                def up_compute(j, T=T, gch=gch, lastg=lastg):
                    w_t, w_r = ws.pop(j)
                    gacc = []
                    for (c0, cwid) in gch:
                        a_t, a_r = acc.next()
                        gacc.append((a_t, a_r, c0, cwid))
                        for k in range(KD):
                            P.add("pe", i_mm(a_t[:, 0:cwid], w_t[:, k, 0:128], hn2T[:, k, c0:c0 + cwid], k == 0, k == KD - 1),
                                  reads=[w_r, hn2T_r], **acc_rw(a_r, k == 0))
                    v_t, v_r = acc.next()
                    for k in range(KD):
                        P.add("pe", i_mm(v_t[:, 0:T], w_t[:, k, 128:256], hn2T[:, k, 1:1 + T], k == 0, k == KD - 1),
                              reads=[w_r, hn2T_r], **acc_rw(v_r, k == 0))
                    g_t, g_r = gs.next()
                    for idx, (a_t, a_r, c0, cwid) in enumerate(gacc):
                        P.add("act", i_act(g_t[:, c0:c0 + cwid], a_t[:, 0:cwid], AF.Copy), reads=[a_r],
                              **acc_rw(g_r, idx == 0))
                    A_t, A_r = cA.next()
                    B_t, B_r = cB.next()
                    P.add("dve", i_ts(A_t[:, 0:T], g_t[:, 1:T + 1], cw[:, j, 1:2], cbv[:, j:j + 1], ALU.mult, ALU.add),
                          reads=[g_r, cw_r, cb_r], writes=[A_r])
                    P.add("dve", i_stt(B_t[:, 0:T], g_t[:, 0:T], cw[:, j, 0:1], A_t[:, 0:T], ALU.mult, ALU.add),
                          reads=[g_r, cw_r, A_r], writes=[B_r])
                    if lastg:
                        gh_t, gh_r = gh.next()
                        P.add("dve", i_ts(gh_t[:, 0:1], g_t[:, T + 1:T + 2], gm[:, 0:1], None, ALU.mult),
                              reads=[g_r, gm_r], writes=[gh_r])
                        P.add("dve", i_stt(A_t[:, 0:T - 1], g_t[:, 2:T + 1], cw[:, j, 2:3], B_t[:, 0:T - 1],
                                           ALU.mult, ALU.add), reads=[g_r, cw_r, B_r], writes=[A_r])
                        P.add("dve", i_stt(A_t[:, T - 1:T], gh_t[:, 0:1], cw[:, j, 2:3], B_t[:, T - 1:T],
                                           ALU.mult, ALU.add), reads=[gh_r, cw_r, B_r], pwrites=[A_r])
                    else:
                        P.add("dve", i_stt(A_t[:, 0:T], g_t[:, 2:T + 2], cw[:, j, 2:3], B_t[:, 0:T], ALU.mult, ALU.add),
                              reads=[g_r, cw_r, B_r], writes=[A_r])
                    P.add("act", i_act(B_t[:, 0:T], A_t[:, 0:T], AF.Gelu), reads=[A_r], writes=[B_r])
                    P.add("dve", i_tt(actT[:, j, 0:T], v_t[:, 0:T], B_t[:, 0:T], ALU.mult),
                          reads=[B_r, v_r], **acc_rw(act_r, j == 0))

                pipeline(NFF, up_load, up_compute, 1)

                ttiles = [(o, min(128, T - o)) for o in range(0, T, 128)]
                fslabs = list(range(0, NFF, KS))
                ditems = [(nb, si) for nb in range(NB) for si in range(len(fslabs))]
                ds = {}
                hs = {}
                dstate = {}

                def dn_load(i, ditems=ditems, fslabs=fslabs, ttiles=ttiles, r0=r0):
                    nb, si = ditems[i]
                    f0 = fslabs[si]
                    fn = min(KS, NFF - f0)
                    d_t, d_r = wdn.next()
                    ds[i] = (d_t, d_r)
                    DMA(d_t[:, 0:fn, :], wdnb[nb, :, f0:f0 + fn, :], reads=[wdn_res[nb]], writes=[d_r], key=d_r)

                def dn_compute(i, ditems=ditems, fslabs=fslabs, ttiles=ttiles, r0=r0):
                    nb, si = ditems[i]
                    f0 = fslabs[si]
                    fn = min(KS, NFF - f0)
                    d_t, d_r = ds.pop(i)
                    if si == max(0, len(fslabs) - 5):
                        hl = []
                        for (o, n) in ttiles:
                            h_t, h_r = hp.next()
                            hl.append((h_t, h_r))
                            row = r0 + o
                            DMA(h_t[0:n, :], h2_s[row + 1:row + 1 + n, nb * 512:(nb + 1) * 512], reads=[h2_res],
                                writes=[h_r], key=h_r)
                        hs[nb] = hl
                    if si == 0:
                        dstate["acc"] = [acc.next() for _ in ttiles]
                    accs_nb = dstate["acc"]
                    for ff in range(fn):
                        f = f0 + ff
                        for ti, (o, n) in enumerate(ttiles):
                            a_t, a_r = accs_nb[ti]
                            P.add("pe", i_mm(a_t[0:n, :], actT[:, f, o:o + n], d_t[:, ff, :], f == 0, f == NFF - 1),
                                  reads=[act_r, d_r], **acc_rw(a_r, f == 0))
                    if si != len(fslabs) - 1:
                        return
                    hl = hs.pop(nb)
                    for ti, (o, n) in enumerate(ttiles):
                        a_t, a_r = accs_nb[ti]
                        h_t, h_r = hl[ti]
                        o_t, o_r = op.next()
                        row = r0 + o
                        P.add("dve", i_tt(o_t[0:n, :], a_t[0:n, :], h_t[0:n, :], ALU.add), reads=[a_r, h_r], writes=[o_r])
                        DMA(y[row:row + n, nb * 512:(nb + 1) * 512], o_t[0:n, :], reads=[o_r], pwrites=[y_res], key=o_r)

                pipeline(len(ditems), dn_load, dn_compute, 3)
            P.emit_phase("ffn")

        ytiles = [(o, min(128, HALF - o)) for o in range(0, HALF, 128)]
        norm_phase("fin", y, ytiles, nfw_d, NORM_EPS, "F", dst_res=y_res, src_res=y_res, dst_rows=y)

        fin = list(P.final_events) + [(s, s.count) for s in P.persist_sems]
        with nc.Block() as block:
            def tail(eng):
                for (sem, val) in fin:
                    eng.wait_ge(sem.h, val)
            block.sync(tail)
    return nc


def prepare_inputs(cfg, x, meta_tokens, norm1_w, w_in, lambda_q1, lambda_k1, lambda_q2, lambda_k2,
                   diff_subln_w, fourier_norm_w, w_out, norm2_w, w_up, conv_w, conv_b, w_down, final_norm_w):
    f32 = np.float32
    bf16 = ml_dtypes.bfloat16
    D, L, NOWN, NTOK, HALF, NMETA, SEQ = cfg.D, cfg.L, cfg.NOWN, cfg.NTOK, cfg.HALF, cfg.NMETA, cfg.SEQ
    NKC, QW, GD, GC, NFF = cfg.NKC, cfg.QW, cfg.GD, cfg.GC, cfg.NFF
    x = np.asarray(x, f32)
    meta = np.asarray(meta_tokens, f32)
    w_in2 = np.ascontiguousarray(np.asarray(w_in, f32)[0])
    w_out2 = np.ascontiguousarray(np.asarray(w_out, f32)[0])
    w_up2 = np.ascontiguousarray(np.asarray(w_up, f32)[0])
    w_down2 = np.ascontiguousarray(np.asarray(w_down, f32)[0])
    bc = lambda v: np.ascontiguousarray(np.broadcast_to(np.asarray(v, f32).reshape(1, -1), (128, np.asarray(v).size)))
    n1w, n2w, nfw = bc(norm1_w[0]), bc(norm2_w[0]), bc(final_norm_w)
    lamv = np.ascontiguousarray(np.broadcast_to(
        np.stack([np.asarray(v, f32)[0] for v in (lambda_q1, lambda_k1, lambda_q2, lambda_k2)])[None], (128, 4, 64)))
    subw = np.ascontiguousarray(np.asarray(diff_subln_w, f32)[0].reshape(128, 1))
    fw = np.ascontiguousarray(np.asarray(fourier_norm_w, f32)[0].reshape(cfg.NG * GC, 128).T)
    cw = np.ascontiguousarray(np.asarray(conv_w, f32)[0].reshape(3, NFF, 128).transpose(2, 1, 0))
    cb = np.ascontiguousarray(np.asarray(conv_b, f32)[0].reshape(NFF, 128).T)
    ident = np.eye(128, dtype=f32).astype(bf16)
    cc = np.arange(GD)
    ang = 2.0 * np.pi * ((cc[:, None] * cc[None, :]) % GD) / GD
    cd = np.stack([np.cos(ang), np.sin(ang)]).reshape(2, GC, 128, GD).transpose(0, 2, 1, 3)
    cdft = np.ascontiguousarray(cd).astype(bf16)

    PADPOS = 1.0e6
    per_t = {}
    for t in range(2):
        own_pos = np.arange(NOWN) + (NMETA - 1) + HALF * t
        if t == 0:
            oth_pos = np.concatenate([np.arange(0, NMETA - 1), np.arange(NMETA + HALF + 1, L), [L]])
        else:
            oth_pos = np.concatenate([np.arange(0, NMETA), np.arange(NMETA, NMETA + HALF - 1)])
        assert len(oth_pos) == cfg.NOTH, (len(oth_pos), cfg.NOTH)
        allpos = np.concatenate([own_pos, oth_pos])
        assert len(allpos) == NTOK and len(np.unique(allpos)) == NTOK
        ispad = allpos >= L
        posq = own_pos.astype(f32)
        posk_flat = np.where(ispad, PADPOS, allpos).astype(f32)
        posk = np.zeros((128, NKC), f32)
        for j, (t0, n) in enumerate(cfg.tiles):
            posk[:n, j] = posk_flat[t0:t0 + n]
        dft = np.zeros((len(cfg.qchunks), NKC * 128, 2, QW), f32)
        pl = allpos.astype(np.int64)
        po = own_pos.astype(np.int64)
        for qi, (q0, w) in enumerate(cfg.qchunks):
            for j, (t0, n) in enumerate(cfg.tiles):
                prod = (pl[t0:t0 + n, None] * po[None, q0:q0 + w]) % L
                a = 2.0 * np.pi * prod / L
                valid = (~ispad[t0:t0 + n])[:, None] & (po[None, q0:q0 + w] < L)
                dft[qi, j * 128:j * 128 + n, 0, :w] = np.where(valid, np.cos(a), 0.0)
                dft[qi, j * 128:j * 128 + n, 1, :w] = np.where(valid, -np.sin(a), 0.0)
        per_t[t] = dict(idx=allpos, posq=np.ascontiguousarray(np.broadcast_to(posq[None], (128, NOWN))),
                        posk=posk, dft=dft.astype(bf16),
                        gmask=np.full((128, 1), 1.0 if t == 0 else 0.0, f32))
    in_maps = []
    for c in range(2 * cfg.B):
        b, t = c // 2, c % 2
        pt = per_t[t]
        table = np.concatenate([meta, x[b], np.zeros((1, D), f32)], axis=0)
        xall = table[np.minimum(pt["idx"], L)]
        in_maps.append(dict(
            xall=xall, posq=pt["posq"], posk=pt["posk"], dft=pt["dft"], cdft=cdft,
            w_in=w_in2, w_out=w_out2, w_up=w_up2, w_down=w_down2, n1w=n1w, n2w=n2w, nfw=nfw,
            lamv=lamv, subw=subw, fw=fw, cw=cw, cb=cb, gmask=pt["gmask"], ident=ident))
    return in_maps


def run(cfg, inputs, trace=False):
    nc = build_program(cfg)
    in_maps = prepare_inputs(cfg, **inputs)
    n = 2 * cfg.B
    res = run_bass_kernel_spmd(nc, in_maps, core_ids=list(range(n)), trace=trace)
    out = np.zeros((cfg.B, cfg.SEQ, cfg.D), np.float32)
    for c in range(n):
        b, t = c // 2, c % 2
        out[b, t * cfg.HALF:(t + 1) * cfg.HALF] = res.results[c]["y"]
    return out, res


def kernel(**inputs):
    cfg = Cfg()
    out, _ = run(cfg, inputs)
    return out
```

```python
import math
from contextlib import ExitStack

import numpy as np
import ml_dtypes

import concourse.bass as bass
import concourse.mybir as mybir
from concourse.bass_utils import run_bass_kernel_spmd

F32 = mybir.dt.float32
BF16 = mybir.dt.bfloat16
AF = mybir.ActivationFunctionType
ALU = mybir.AluOpType
AX = mybir.AxisListType

NORM_EPS = 1e-6
SUBLN_EPS = 1e-5


class Cfg:
    def __init__(s, D=4096, SEQ=4096, B=4, NMETA=16, H=16, NG=4, GD=512, DFF=11008):
        s.D, s.SEQ, s.B, s.NMETA, s.H, s.NG, s.GD, s.DFF = D, SEQ, B, NMETA, H, NG, GD, DFF
        s.QK = H * 128
        s.ATT = H * 128
        s.FW = NG * GD
        s.IN = 2 * s.QK + s.ATT + s.FW
        s.MIX = s.ATT + s.FW
        assert s.MIX == D
        s.L = NMETA + SEQ
        s.HALF = SEQ // 2
        s.NOWN = s.HALF + 2
        s.NTOK = s.L + 1
        s.NOTH = s.NTOK - s.NOWN
        s.KD = D // 128
        s.NFF = DFF // 128
        s.GC = GD // 128
        assert D % 512 == 0 and DFF % 128 == 0 and s.IN % 256 == 0 and GD % 128 == 0
        s.tiles = []
        for base, n in ((0, s.NOWN), (s.NOWN, s.NOTH)):
            o = 0
            while o < n:
                m = min(128, n - o)
                s.tiles.append((base + o, m))
                o += m
        s.NKC = len(s.tiles)
        s.n_own_tiles = sum(1 for (t0, n) in s.tiles if t0 < s.NOWN)
        s.qchunks = split(s.NOWN, 512)
        s.QW = max(w for _, w in s.qchunks)
        s.lam_init = 0.8 - 0.6 * math.exp(-0.3 * 0)
        s.slopes = [2.0 ** (-8.0 * (h + 1) / H) for h in range(H)]


def split(n, mx):
    k = (n + mx - 1) // mx
    base, rem = divmod(n, k)
    out, o = [], 0
    for i in range(k):
        w = base + (1 if i < rem else 0)
        out.append((o, w))
        o += w
    return out


class Res:
    __slots__ = ("name", "writers", "readers", "sem")

    def __init__(self, name):
        self.name = name
        self.writers = []
        self.readers = []
        self.sem = None


class Sem:
    __slots__ = ("h", "count")

    def __init__(self, h):
        self.h = h
        self.count = 0


class Task:
    __slots__ = ("eng", "fn", "deps", "signal", "sem", "val", "is_dma", "idx")


COMPUTE = ("pe", "act", "dve", "pool")
ENGS = ("pe", "act", "dve", "pool", "sp")


class Prog:
    def __init__(self, nc, gstack):
        self.nc = nc
        self.gstack = gstack
        self.nsem = 0
        self.eng_sem = {e: self._new_sem(e) for e in COMPUTE}
        self.dma_pool = []
        self.dma_pool_used = 0
        self.tasks = {e: [] for e in ENGS}
        self.all = []
        self.barrier_events = []
        self.phase_dma_sems = []
        self.persist_sems = []

    def _new_sem(self, name):
        self.nsem += 1
        h = self.gstack.enter_context(self.nc.semaphore(f"s{self.nsem}_{name}"))
        return Sem(h)

    def dma_sem(self):
        if self.dma_pool_used < len(self.dma_pool):
            s = self.dma_pool[self.dma_pool_used]
        else:
            s = self._new_sem("dma")
            self.dma_pool.append(s)
        self.dma_pool_used += 1
        return s

    def persistent_sem(self):
        s = self._new_sem("pdma")
        self.persist_sems.append(s)
        return s

    def add(self, eng, fn, reads=(), writes=(), pwrites=(), key=None, qeng=None):
        t = Task()
        t.is_dma = eng == "dma"
        t.eng = qeng if t.is_dma else eng
        t.fn = fn
        deps = []
        for r in reads:
            deps.extend(r.writers)
        for w in writes:
            deps.extend(w.writers)
            deps.extend(w.readers)
        for w in pwrites:
            deps.extend(w.readers)
        seen = set()
        t.deps = []
        latest = {}
        for d in deps:
            if id(d) in seen:
                continue
            seen.add(id(d))
            if (not d.is_dma) and (not t.is_dma) and d.eng == "pe" and t.eng == "pe":
                continue
            if d.is_dma:
                t.deps.append(d)
                d.signal = True
            else:
                if d.eng not in latest or d.idx > latest[d.eng].idx:
                    latest[d.eng] = d
        for d in latest.values():
            t.deps.append(d)
            d.signal = True
        t.signal = t.is_dma
        if t.is_dma:
            if key.sem is None:
                key.sem = self.dma_sem()
                self.phase_dma_sems.append(key.sem)
            t.sem = key.sem
            key.sem.count += 16
            t.val = key.sem.count
        else:
            t.sem = None
            t.val = None
        for w in writes:
            w.writers = [t]
            w.readers = []
        for w in pwrites:
            w.writers.append(t)
        for r in reads:
            r.readers.append(t)
        t.idx = len(self.all)
        self.tasks[t.eng].append(t)
        self.all.append(t)
        return t

    def emit_phase(self, name):
        nc = self.nc
        for e in COMPUTE:
            if self.tasks[e]:
                self.tasks[e][-1].signal = True
        for e in COMPUTE:
            s = self.eng_sem[e]
            for t in self.tasks[e]:
                if t.is_dma:
                    continue
                if t.signal:
                    if s.count >= 30000:
                        s = self._new_sem(e)
                        self.eng_sem[e] = s
                    s.count += 1
                    t.sem = s
                    t.val = s.count
        barrier = list(self.barrier_events)

        def run(engname):
            def body(eng):
                waited = {}

                def wait(sem, val):
                    if waited.get(id(sem), 0) >= val:
                        return
                    waited[id(sem)] = val
                    eng.wait_ge(sem.h, val)

                for (sem, val) in barrier:
                    wait(sem, val)
                for t in self.tasks[engname]:
                    need = {}
                    for d in t.deps:
                        if id(d.sem) not in need or need[id(d.sem)][1] < d.val:
                            need[id(d.sem)] = (d.sem, d.val)
                    for (sem_, val_) in need.values():
                        wait(sem_, val_)
                    ins = t.fn(eng)
                    if t.signal:
                        ins.then_inc(t.sem.h, 16 if t.is_dma else 1)
            return body

        with nc.Block() as block:
            block.tensor(run("pe"))
            block.scalar(run("act"))
            block.vector(run("dve"))
            block.gpsimd(run("pool"))
            block.sync(run("sp"))
        ev = []
        for e in COMPUTE:
            if self.tasks[e]:
                lt = [t for t in self.tasks[e] if not t.is_dma]
                if lt:
                    ev.append((lt[-1].sem, lt[-1].val))
        for s in self.phase_dma_sems:
            ev.append((s, s.count))
        self.barrier_events = ev
        self.final_events = ev
        self.tasks = {e: [] for e in ENGS}
        self.phase_dma_sems = []
        self.dma_pool_used = 0

    def forget(self, *ress):
        for r in ress:
            r.writers = []
            r.readers = []


class Ring:
    def __init__(self, nc, stack, name, shape, dtype, n, psum=False):
        self.t = []
        self.r = []
        for i in range(n):
            if psum:
                self.t.append(stack.enter_context(nc.psum_tensor(f"{name}{i}", shape, dtype)))
            else:
                self.t.append(stack.enter_context(nc.sbuf_tensor(f"{name}{i}", shape, dtype)))
            self.r.append(Res(f"{name}{i}"))
        self.i = 0
        self.n = n

    def next(self):
        k = self.i % self.n
        self.i += 1
        r = self.r[k]
        if r.writers and not r.readers:
            raise RuntimeError(f"ring {r.name} oversubscribed: slot handed out again before its data was read")
        return self.t[k], r


def sb(nc, stack, name, shape, dtype):
    return stack.enter_context(nc.sbuf_tensor(name, shape, dtype)), Res(name)


def i_dma(out, in_):
    return lambda e: e.dma_start(out=out, in_=in_)


def i_mm(out, lhsT, rhs, start, stop):
    return lambda e: e.matmul(out=out, lhsT=lhsT, rhs=rhs, start=start, stop=stop)


def i_tr(out, in_, ident):
    return lambda e: e.transpose(out=out, in_=in_, identity=ident)


def i_act(out, in_, func, **kw):
    return lambda e: e.activation(out=out, in_=in_, func=func, **kw)


def i_ts(out, in0, s1, s2, op0, op1=None):
    if op1 is None:
        return lambda e: e.tensor_scalar(out=out, in0=in0, scalar1=s1, scalar2=None, op0=op0)
    return lambda e: e.tensor_scalar(out=out, in0=in0, scalar1=s1, scalar2=s2, op0=op0, op1=op1)


def i_stt(out, in0, scalar, in1, op0, op1):
    return lambda e: e.scalar_tensor_tensor(out=out, in0=in0, scalar=scalar, in1=in1, op0=op0, op1=op1)


def i_tt(out, in0, in1, op):
    return lambda e: e.tensor_tensor(out=out, in0=in0, in1=in1, op=op)


def i_cp(out, in_):
    return lambda e: e.tensor_copy(out=out, in_=in_)


def i_recip(out, in_):
    return lambda e: e.reciprocal(out=out, in_=in_)


def i_memset(ap, v):
    return lambda e: e.memset(ap, v)


def i_reduce(out, in_, axis, op):
    return lambda e: e.tensor_reduce(out=out, in_=in_, axis=axis, op=op)


def build_program(cfg):
    nc = bass.Bass("TRN2", target_bir_lowering=False)
    D, KD, NTOK, NOWN, HALF = cfg.D, cfg.KD, cfg.NTOK, cfg.NOWN, cfg.HALF
    H, NG, GD, GC, NFF, NKC, QW = cfg.H, cfg.NG, cfg.GD, cfg.GC, cfg.NFF, cfg.NKC, cfg.QW
    IN, ATT, FW, DFF = cfg.IN, cfg.ATT, cfg.FW, cfg.DFF
    NQC = len(cfg.qchunks)
    NB = D // 512

    def din(name, shape, dt=F32):
        return nc.dram_tensor(name, list(shape), dt, kind="ExternalInput").ap()

    def dscr(name, shape, dt):
        return nc.dram_tensor(name, list(shape), dt, kind="Internal").ap()

    xall = din("xall", [NTOK, D])
    posq_d = din("posq", [128, NOWN])
    posk_d = din("posk", [128, NKC])
    dft_d = din("dft", [NQC, NKC * 128, 2, QW], BF16)
    cdft_d = din("cdft", [2, 128, GC, GD], BF16)
    w_in = din("w_in", [D, IN])
    w_out = din("w_out", [cfg.MIX, D])
    w_up = din("w_up", [D, 2 * DFF])
    w_down = din("w_down", [DFF, D])
    n1w_d = din("n1w", [128, D])
    n2w_d = din("n2w", [128, D])
    nfw_d = din("nfw", [128, D])
    lam_d = din("lamv", [128, 4, 64])
    subw_d = din("subw", [128, 1])
    fw_d = din("fw", [128, NG * GC])
    cw_d = din("cw", [128, NFF, 3])
    cb_d = din("cb", [128, NFF])
    mask_d = din("gmask", [128, 1])
    ident_d = din("ident", [128, 128], BF16)
    y = nc.dram_tensor("y", [HALF, D], F32, kind="ExternalOutput").ap()

    winb = dscr("winb", [IN // 256, 128, KD, 256], BF16)
    woutb = dscr("woutb", [NB, 128, KD, 512], BF16)
    wupb = dscr("wupb", [NFF, 128, KD, 256], BF16)
    wdnb = dscr("wdnb", [NB, 128, NFF, 512], BF16)
    hnT_s = dscr("hnT_s", [D, NTOK], BF16)
    KT_s = dscr("KT_s", [H, 128, NTOK], BF16)
    QT_s = dscr("QT_s", [H, 128, NOWN], BF16)
    V_s = dscr("V_s", [H, 128, NKC, 128], BF16)
    G_s = dscr("G_s", [2, NG, 128, NKC, GD], BF16)
    mixT_s = dscr("mixT_s", [cfg.MIX, NOWN], BF16)
    h2_s = dscr("h2_s", [NOWN, D], F32)
    hn2T_s = dscr("hn2T_s", [D, NOWN], BF16)

    def kp(ap):
        return ap.rearrange("(k p) t -> p k t", p=128)

    gstack = ExitStack()
    with gstack:
        P = Prog(nc, gstack)

        def DMA(out, in_, reads=(), writes=(), pwrites=(), key=None, q="sp"):
            return P.add("dma", i_dma(out, in_), reads=reads, writes=writes, pwrites=pwrites, key=key, qeng=q)

        def acc_rw(r, first):
            return dict(writes=[r], pwrites=[]) if first else dict(writes=[], pwrites=[r])

        nqb = cfg.QK // 256
        nvb = ATT // 256
        cb_order = (list(range(nqb, 2 * nqb)) + list(range(2 * nqb, 2 * nqb + nvb))
                    + list(range(2 * nqb + nvb, IN // 256)) + list(range(0, nqb)))
        KG = min(8, KD)
        win_res = {}
        wout_res = []
        wup_res = [None] * NFF
        wdn_res = []

        def prep_win(part):
            half = (len(cb_order) // 8) * 2
            rng = range(0, half, 2) if part == 0 else range(half, len(cb_order), 2)
            for gi in rng:
                r = Res(f"winb{gi}")
                r.sem = P.persistent_sem()
                for cb in cb_order[gi:gi + 2]:
                    win_res[cb] = r
                    for kg in range(0, KD, KG):
                        DMA(winb[cb, :, kg:kg + KG, :], kp(w_in[kg * 128:(kg + KG) * 128, cb * 256:(cb + 1) * 256]),
                            pwrites=[r], key=r, q="pool")

        def prep_wout():
            for nb in range(NB):
                r = Res(f"woutb{nb}")
                r.sem = P.persistent_sem()
                wout_res.append(r)
                for kg in range(0, KD, KG):
                    DMA(woutb[nb, :, kg:kg + KG, :], kp(w_out[kg * 128:(kg + KG) * 128, nb * 512:(nb + 1) * 512]),
                        pwrites=[r], key=r, q="pool")

        def prep_wup():
            JG = 16
            for j0 in range(0, NFF, JG):
                r = Res(f"wupb{j0}")
                r.sem = P.persistent_sem()
                for j in range(j0, min(NFF, j0 + JG)):
                    wup_res[j] = r
                    for kg in range(0, KD, KG):
                        for half, cbase in ((0, 0), (1, DFF)):
                            DMA(wupb[j, :, kg:kg + KG, half * 128:(half + 1) * 128],
                                kp(w_up[kg * 128:(kg + KG) * 128, cbase + j * 128:cbase + (j + 1) * 128]),
                                pwrites=[r], key=r, q="pool")

        def prep_wdn():
            FG = 8
            for nb in range(NB):
                r = Res(f"wdnb{nb}")
                r.sem = P.persistent_sem()
                wdn_res.append(r)
                for f0 in range(0, NFF, FG):
                    f1 = min(NFF, f0 + FG)
                    DMA(wdnb[nb, :, f0:f1, :], kp(w_down[f0 * 128:f1 * 128, nb * 512:(nb + 1) * 512]),
                        pwrites=[r], key=r, q="pool")

        def pipeline(n, load, compute, pf):
            for i in range(min(pf, n)):
                load(i)
            for i in range(n):
                if i + pf < n:
                    load(i + pf)
                compute(i)

        prep_win(0)

        hnT_res = Res("hnT_s")
        KT_res = [Res(f"KT{h}") for h in range(H)]
        QT_res = [Res(f"QT{h}") for h in range(H)]
        V_res = Res("V_s")
        G_res = [Res(f"G{g}") for g in range(NG)]
        mixA_res = [Res(f"mixA{h}") for h in range(H)]
        mixF_res = [Res(f"mixF{g}") for g in range(NG)]
        h2_res = Res("h2_s")
        hn2T_res = Res("hn2T_s")
        y_res = Res("y")

        evac_ctr = [0]

        def evac(out_ap, in_ap, reads, writes=(), pwrites=()):
            evac_ctr[0] += 1
            if evac_ctr[0] % 2:
                P.add("act", i_act(out_ap, in_ap, AF.Copy), reads=reads, writes=writes, pwrites=pwrites)
            else:
                P.add("dve", i_cp(out_ap, in_ap), reads=reads, writes=writes, pwrites=pwrites)

        def norm_phase(name, src, rows_tiles, wb_d, eps, mode, dstT=None, dst_res=None, src_res=None,
                       dst_rows=None, extra=None):
            with ExitStack() as st:
                wb, wb_r = sb(nc, st, name + "wb", [128, D], F32)
                xt = Ring(nc, st, name + "x", [128, D], F32, 4)
                ot = Ring(nc, st, name + "o", [128, D], BF16 if mode == "T" else F32, 2)
                ss = Ring(nc, st, name + "ss", [128, 1], F32, 4)
                rs = Ring(nc, st, name + "rs", [128, 1], F32, 4)
                ln = Ring(nc, st, name + "ln", [128, 1], F32, 4)
                epsc, epsc_r = sb(nc, st, name + "eps", [128, 1], F32)
                P.add("dve", i_memset(epsc[:], eps), writes=[epsc_r])
                DMA(wb[:], wb_d[:, :], writes=[wb_r], key=wb_r)
                if mode == "T":
                    ident, ident_r = sb(nc, st, name + "id", [128, 128], BF16)
                    tT = Ring(nc, st, name + "tT", [128, KD, 528], BF16, 2)
                    pst = Ring(nc, st, name + "ps", [128, 8, 128], BF16, 8, psum=True)
                    DMA(ident[:], ident_d[:, :], writes=[ident_r], key=ident_r)
                if extra is not None:
                    extra()
                groups, cur = [], []
                for (r0, n) in rows_tiles:
                    if cur and (cur[-1][0] + cur[-1][1] != r0 or (r0 + n - cur[0][0]) > 528):
                        groups.append(cur)
                        cur = []
                    cur.append((r0, n))
                if cur:
                    groups.append(cur)
                inv_sqrt_d = 1.0 / math.sqrt(D)
                sread = [src_res] if src_res is not None else []
                items = []
                for gi, grp in enumerate(groups):
                    for ti, (r0, n) in enumerate(grp):
                        items.append((r0, n, gi, ti == 0, ti == len(grp) - 1))
                xs = {}
                state = {}

                def load(i):
                    r0, n = items[i][0], items[i][1]
                    x_t, x_r = xt.next()
                    xs[i] = (x_t, x_r)
                    DMA(x_t[0:n, :], src[r0:r0 + n, :], reads=sread, writes=[x_r], key=x_r)

                def compute(i):
                    r0, n, gi, gfirst, glast = items[i]
                    grp = groups[gi]
                    g0 = grp[0][0]
                    x_t, x_r = xs.pop(i)
                    o_t, o_r = ot.next()
                    s_t, s_r = ss.next()
                    r_t, r_r = rs.next()
                    l_t, l_r = ln.next()
                    if mode == "T" and gfirst:
                        state["tT"] = tT.next()
                    P.add("act", i_act(o_t[0:n, :], x_t[0:n, :], AF.Square, scale=inv_sqrt_d,
                                       accum_out=s_t[0:n, :]), reads=[x_r], writes=[o_r, s_r])
                    P.add("act", i_act(l_t[0:n, :], s_t[0:n, :], AF.Ln, bias=epsc[0:n, 0:1]), reads=[s_r, epsc_r],
                          writes=[l_r])
                    P.add("act", i_act(r_t[0:n, :], l_t[0:n, :], AF.Exp, scale=-0.5), reads=[l_r], writes=[r_r])
                    P.add("dve", i_stt(o_t[0:n, :], x_t[0:n, :], r_t[0:n, 0:1], wb[0:n, :], ALU.mult, ALU.mult),
                          reads=[x_r, r_r, wb_r], writes=[o_r])
                    if mode == "F":
                        DMA(dst_rows[r0:r0 + n, :], o_t[0:n, :], reads=[o_r], pwrites=[dst_res], key=o_r)
                        return
                    tTt, tTr = state["tT"]
                    c0 = r0 - g0
                    for kb in range(0, KD, 8):
                        kn = min(8, KD - kb)
                        p_t, p_r = pst.next()
                        for kk in range(kn):
                            k = kb + kk
                            P.add("pe", i_tr(p_t[:, kk, 0:n], o_t[0:n, k * 128:(k + 1) * 128], ident[0:n, 0:n]),
                                  reads=[o_r, ident_r], **acc_rw(p_r, kk == 0))
                        evac(tTt[:, kb:kb + kn, c0:c0 + n], p_t[:, 0:kn, 0:n], reads=[p_r], pwrites=[tTr])
                    if glast:
                        gn = grp[-1][0] + grp[-1][1] - g0
                        DMA(kp(dstT[:, g0:g0 + gn]), tTt[:, :, 0:gn], reads=[tTr], pwrites=[dst_res], key=tTr)

                pipeline(len(items), load, compute, 3)
                P.emit_phase(name)

        norm_phase("n1", xall, cfg.tiles, n1w_d, NORM_EPS, "T", dstT=hnT_s, dst_res=hnT_res)

        supers, cur = [], []
        for (t0, n) in cfg.tiles:
            own = t0 < NOWN
            if cur and ((cur[0][0] < NOWN) != own or (t0 + n - cur[0][0]) > 1040):
                supers.append(cur)
                cur = []
            cur.append((t0, n))
        if cur:
            supers.append(cur)
        TS = max(s[-1][0] + s[-1][1] - s[0][0] for s in supers)
        tile_index = {t0: i for i, (t0, n) in enumerate(cfg.tiles)}
        with ExitStack() as st:
            hnTr = Ring(nc, st, "pj_hnT", [128, KD, TS], BF16, 2)
            wblk = Ring(nc, st, "pj_w", [128, KD, 256], BF16, 3)
            cdft, cdft_r = sb(nc, st, "pj_cdft", [128, 2, GC, GD], BF16)
            fT, fT_r = sb(nc, st, "pj_fT", [128, GC, TS], BF16)
            stg = Ring(nc, st, "pj_stg", [128, 512], BF16, 4)
            acc = Ring(nc, st, "pj_acc", [128, 512], F32, 8, psum=True)
            prep_win(1)
            prep_wout()
            prep_wup()
            for cs in range(2):
                DMA(cdft[:, cs, :, :], cdft_d[cs], pwrites=[cdft_r], key=cdft_r)
            zt, zt_r = sb(nc, st, "pj_zero", [128, max(ATT, FW)], BF16)
            P.add("dve", i_memset(zt[:], 0.0), writes=[zt_r])
            for ti, (t0, n) in enumerate(cfg.tiles):
                if n < 128:
                    DMA(V_s[:, n:128, ti, :].rearrange("h p c -> p h c"),
                        zt[n:128, 0:ATT].rearrange("p (h c) -> p h c", h=H), reads=[zt_r], pwrites=[V_res], key=zt_r)
                    for cs in range(2):
                        DMA(G_s[cs, :, n:128, ti, :].rearrange("g p c -> p g c"),
                            zt[n:128, 0:FW].rearrange("p (g c) -> p g c", g=NG), reads=[zt_r],
                            pwrites=G_res, key=zt_r)
            items = []
            for si, stiles in enumerate(supers):
                own = stiles[0][0] < NOWN
                blocks = [c for c in cb_order if own or c >= nqb]
                for bi, cb in enumerate(blocks):
                    items.append((si, cb, bi == 0))
            ws = {}
            hs = {}

            def pj_load(i):
                si, cb, sfirst = items[i]
                if sfirst:
                    stiles = supers[si]
                    s0 = stiles[0][0]
                    sn = stiles[-1][0] + stiles[-1][1] - s0
                    h_t, h_r = hnTr.next()
                    hs[si] = (h_t, h_r)
                    DMA(h_t[:, :, 0:sn], kp(hnT_s[:, s0:s0 + sn]), reads=[hnT_res], writes=[h_r], key=h_r)
                w_t, w_r = wblk.next()
                ws[i] = (w_t, w_r)
                DMA(w_t[:], winb[cb], reads=[win_res[cb]], writes=[w_r], key=w_r)

            def pj_compute(i):
                si, cb, sfirst = items[i]
                stiles = supers[si]
                s0 = stiles[0][0]
                sn = stiles[-1][0] + stiles[-1][1] - s0
                mch = split(sn, 512)
                hnT, hnT_r = hs[si]
                w_t, w_r = ws.pop(i)
                col0 = cb * 256
                if col0 < 2 * cfg.QK:
                    isq = col0 < cfg.QK
                    for hh in range(2):
                        h = (col0 - (0 if isq else cfg.QK)) // 128 + hh
                        for (m0, mw) in mch:
                            a_t, a_r = acc.next()
                            for k in range(KD):
                                P.add("pe", i_mm(a_t[:, 0:mw], w_t[:, k, hh * 128:(hh + 1) * 128],
                                                 hnT[:, k, m0:m0 + mw], k == 0, k == KD - 1),
                                      reads=[w_r, hnT_r], **acc_rw(a_r, k == 0))
                            g_t, g_r = stg.next()
                            evac(g_t[:, 0:mw], a_t[:, 0:mw], reads=[a_r], writes=[g_r])
                            if isq:
                                dst, dres = QT_s[h, :, s0 + m0:s0 + m0 + mw], QT_res[h]
                            else:
                                dst, dres = KT_s[h, :, s0 + m0:s0 + m0 + mw], KT_res[h]
                            DMA(dst, g_t[:, 0:mw], reads=[g_r], pwrites=[dres], key=g_r)
                elif col0 < 2 * cfg.QK + ATT:
                    vc0 = col0 - 2 * cfg.QK
                    for (t0, n) in stiles:
                        a_t, a_r = acc.next()
                        for k in range(KD):
                            P.add("pe", i_mm(a_t[0:n, 0:256], hnT[:, k, t0 - s0:t0 - s0 + n], w_t[:, k, :],
                                             k == 0, k == KD - 1), reads=[w_r, hnT_r], **acc_rw(a_r, k == 0))
                        g_t, g_r = stg.next()
                        evac(g_t[0:n, 0:256], a_t[0:n, 0:256], reads=[a_r], writes=[g_r])
                        ti = tile_index[t0]
                        hv0 = vc0 // 128
                        DMA(V_s[hv0:hv0 + 2, 0:n, ti, :].rearrange("h p c -> p h c"),
                            g_t[0:n, 0:256].rearrange("p (h c) -> p h c", h=2), reads=[g_r],
                            pwrites=[V_res], key=g_r)
                else:
                    fc0 = col0 - 2 * cfg.QK - ATT
                    g = fc0 // GD
                    cc0 = (fc0 % GD) // 128
                    for hh in range(2):
                        for (m0, mw) in mch:
                            a_t, a_r = acc.next()
                            for k in range(KD):
                                P.add("pe", i_mm(a_t[:, 0:mw], w_t[:, k, hh * 128:(hh + 1) * 128],
                                                 hnT[:, k, m0:m0 + mw], k == 0, k == KD - 1),
                                      reads=[w_r, hnT_r], **acc_rw(a_r, k == 0))
                            evac(fT[:, cc0 + hh, m0:m0 + mw], a_t[:, 0:mw], reads=[a_r], pwrites=[fT_r])
                    if cc0 + 2 >= GC:
                        for (t0, n) in stiles:
                            ti = tile_index[t0]
                            for cs in range(2):
                                a_t, a_r = acc.next()
                                for c in range(GC):
                                    P.add("pe", i_mm(a_t[0:n, 0:GD], fT[:, c, t0 - s0:t0 - s0 + n],
                                                     cdft[:, cs, c, :], c == 0, c == GC - 1),
                                          reads=[fT_r, cdft_r], **acc_rw(a_r, c == 0))
                                g_t, g_r = stg.next()
                                evac(g_t[0:n, 0:GD], a_t[0:n, 0:GD], reads=[a_r], writes=[g_r])
                                DMA(G_s[cs, g, 0:n, ti, :], g_t[0:n, 0:GD],
                                    reads=[g_r], pwrites=[G_res[g]], key=g_r)

            pipeline(len(items), pj_load, pj_compute, 2)
            P.emit_phase("proj")

        aq = split(NOWN, 256)
        AW = max(w for _, w in aq)
        NAQ = len(aq)
        with ExitStack() as st:
            posq, posq_r = sb(nc, st, "at_posq", [128, NOWN], F32)
            posk, posk_r = sb(nc, st, "at_posk", [128, NKC], F32)
            lamv, lamv_r = sb(nc, st, "at_lamv", [128, 4, 64], F32)
            lprod, lprod_r = sb(nc, st, "at_lprod", [128, 2, 64], F32)
            lsum, lsum_r = sb(nc, st, "at_lsum", [128, 2], F32)
            lexp, lexp_r = sb(nc, st, "at_lexp", [128, 2], F32)
            nlam, nlam_r = sb(nc, st, "at_nlam", [128, 1], F32)
            subw0, subw0_r = sb(nc, st, "at_subw0", [128, 1], F32)
            subw, subw_r = sb(nc, st, "at_subw", [128, 1], F32)
            onesb, onesb_r = sb(nc, st, "at_ones", [128, 128], BF16)
            Dt = [st.enter_context(nc.sbuf_tensor(f"at_D{b}", [128, NKC, AW], F32)) for b in range(2)]
            D_r = [[Res(f"D{b}_{j}") for j in range(NKC)] for b in range(2)]
            KT = Ring(nc, st, "at_KT", [128, NTOK], BF16, 3)
            Vt = Ring(nc, st, "at_V", [128, NKC, 128], BF16, 3)
            QT = Ring(nc, st, "at_QT", [128, 2 * AW], BF16, 3)
            tmp = Ring(nc, st, "at_tmp", [128, 2 * AW], F32, 3)
            Et = Ring(nc, st, "at_E", [128, 2 * AW], BF16, 4)
            LAG = 4
            Sp = Ring(nc, st, "at_S", [128, 512], F32, LAG, psum=True)
            accN = [st.enter_context(nc.psum_tensor(f"at_accN{i}", [128, 512], F32)) for i in range(2)]
            accD = [st.enter_context(nc.psum_tensor(f"at_accD{i}", [128, 512], F32)) for i in range(2)]
            accN_r = [Res(f"at_accN{i}") for i in range(2)]
            accD_r = [Res(f"at_accD{i}") for i in range(2)]
            lden = Ring(nc, st, "at_lden", [128, 2 * AW], F32, 2)
            rden = Ring(nc, st, "at_rden", [128, 2 * AW], F32, 2)
            a12 = Ring(nc, st, "at_a12", [128, 2 * AW], F32, 2)
            attn = Ring(nc, st, "at_attn", [128, AW], F32, 3)
            sq = Ring(nc, st, "at_sq", [128, AW], BF16, 3)
            rstd = Ring(nc, st, "at_rstd", [128, AW], F32, 2)
            ostg = Ring(nc, st, "at_o", [128, AW], BF16, 3)
            lnr = Ring(nc, st, "at_ln", [128, AW], F32, 2)
            epsc, epsc_r = sb(nc, st, "at_eps", [128, 1], F32)
            nposk, nposk_r = sb(nc, st, "at_nposk", [128, NKC], F32)
            P.add("dve", i_memset(epsc[:], SUBLN_EPS), writes=[epsc_r])

            DMA(posq[:], posq_d[:, :], writes=[posq_r], key=posq_r)
            DMA(posk[:], posk_d[:, :], writes=[posk_r], key=posk_r)
            P.add("dve", i_ts(nposk[:, :], posk[:, :], -1.0, None, ALU.mult), reads=[posk_r], writes=[nposk_r])
            DMA(lamv[:], lam_d[:, :, :], writes=[lamv_r], key=lamv_r)
            DMA(subw0[:], subw_d[:, :], writes=[subw0_r], key=subw0_r)
            P.add("pool", i_memset(onesb[:], 1.0), writes=[onesb_r])
            P.add("dve", i_tt(lprod[:, 0, :], lamv[:, 0, :], lamv[:, 1, :], ALU.mult), reads=[lamv_r], writes=[lprod_r])
            P.add("dve", i_tt(lprod[:, 1, :], lamv[:, 2, :], lamv[:, 3, :], ALU.mult), reads=[lamv_r], pwrites=[lprod_r])
            P.add("dve", i_reduce(lsum[:, :], lprod[:, :, :], AX.X, ALU.add), reads=[lprod_r], writes=[lsum_r])
            P.add("act", i_act(lexp[:, :], lsum[:, :], AF.Exp), reads=[lsum_r], writes=[lexp_r])
            P.add("dve", i_stt(nlam[:, :], lexp[:, 1:2], -cfg.lam_init, lexp[:, 0:1], ALU.add, ALU.subtract),
                  reads=[lexp_r], writes=[nlam_r])
            P.add("dve", i_ts(subw[:, :], subw0[:, :], (1.0 - cfg.lam_init), None, ALU.mult),
                  reads=[subw0_r], writes=[subw_r])

            inv_sqrt_dv = 1.0 / math.sqrt(128.0)

            def v2(t, n0, n1, w):
                return t[n0:n1, 0:2 * AW].rearrange("p (c a) -> p c a", c=2)[:, :, 0:w]

            items = [(qi, h) for qi in range(NAQ) for h in range(H)]
            NI = len(items)
            ld = {}

            def at_load(i):
                qi, h = items[i]
                q0, w = aq[qi]
                k_t, k_r = KT.next()
                v_t, v_r = Vt.next()
                q_t, q_r = QT.next()
                ld[i] = (k_t, k_r, v_t, v_r, q_t, q_r)
                DMA(k_t[:], KT_s[h], reads=[KT_res[h]], writes=[k_r], key=k_r)
                DMA(v_t[:], V_s[h],
                    reads=[V_res], writes=[v_r], key=v_r)
                DMA(q_t[0:64, 0:w], QT_s[h, 0:64, q0:q0 + w], reads=[QT_res[h]], writes=[q_r], key=q_r)
                DMA(q_t[64:128, AW:AW + w], QT_s[h, 64:128, q0:q0 + w], reads=[QT_res[h]], pwrites=[q_r], key=q_r)

            def gen_D(qi, js):
                q0, w = aq[qi]
                b = qi % 2
                for j in js:
                    t0, nj = cfg.tiles[j]
                    P.add("act", i_act(Dt[b][0:nj, j, 0:w], posq[0:nj, q0:q0 + w], AF.Abs, bias=nposk[0:nj, j:j + 1]),
                          reads=[posq_r, nposk_r], writes=[D_r[b][j]])

            G = [(i, j) for i in range(NI) for j in range(NKC)]
            NGI = len(G)
            sdict = {}

            def qk(g):
                i, j = G[g]
                qi, h = items[i]
                q0, w = aq[qi]
                k_t, k_r, v_t, v_r, q_t, q_r = ld[i]
                t0, nj = cfg.tiles[j]
                s_t, s_r = Sp.next()
                P.add("pe", i_mm(s_t[0:nj, 0:AW + w], k_t[:, t0:t0 + nj], q_t[:, 0:AW + w], True, True),
                      reads=[k_r, q_r], writes=[s_r])
                sdict[g] = (s_t, s_r)

            def epilogue_stages(i):
                qi, h = items[i]
                q0, w = aq[qi]
                s = i % 2
                ld_t, ld_r = lden.next()
                rd_t, rd_r = rden.next()
                a_t, a_r = a12.next()
                at_t, at_r = attn.next()
                sq_t, sq_r = sq.next()
                rs_t, rs_r = rstd.next()
                o_t, o_r = ostg.next()
                ln_t, ln_r = lnr.next()

                def s1():
                    P.add("act", i_act(v2(ld_t, 0, 128, w), v2(accD[s], 0, 128, w), AF.Ln), reads=[accD_r[s]], writes=[ld_r])
                    P.add("act", i_act(v2(rd_t, 0, 128, w), v2(ld_t, 0, 128, w), AF.Exp, scale=-1.0), reads=[ld_r], writes=[rd_r])

                def s2():
                    P.add("dve", i_tt(v2(a_t, 0, 128, w), v2(accN[s], 0, 128, w), v2(rd_t, 0, 128, w), ALU.mult),
                          reads=[accN_r[s], rd_r], writes=[a_r])
                    P.add("dve", i_stt(at_t[:, 0:w], a_t[:, AW:AW + w], nlam[:, 0:1], a_t[:, 0:w], ALU.mult, ALU.add),
                          reads=[a_r, nlam_r], writes=[at_r])

                def s3():
                    P.add("act", i_act(sq_t[:, 0:w], at_t[:, 0:w], AF.Square, scale=inv_sqrt_dv),
                          reads=[at_r], writes=[sq_r])

                def s4():
                    P.add("pe", i_mm(accD[s][:, 0:w], onesb[:, :], sq_t[:, 0:w], True, True),
                          reads=[sq_r, onesb_r], writes=[accD_r[s]])

                def s5():
                    P.add("act", i_act(ln_t[:, 0:w], accD[s][:, 0:w], AF.Ln, bias=epsc[:, 0:1]),
                          reads=[accD_r[s], epsc_r], writes=[ln_r])
                    P.add("act", i_act(rs_t[:, 0:w], ln_t[:, 0:w], AF.Exp, scale=-0.5), reads=[ln_r], writes=[rs_r])

                def s6():
                    P.add("dve", i_stt(o_t[:, 0:w], at_t[:, 0:w], subw[:, 0:1], rs_t[:, 0:w], ALU.mult, ALU.mult),
                          reads=[at_r, rs_r, subw_r], writes=[o_r])
                    DMA(mixT_s[h * 128:(h + 1) * 128, q0:q0 + w], o_t[:, 0:w], reads=[o_r], pwrites=[mixA_res[h]],
                        key=o_r)

                return [(2, s1), (4, s2), (6, s3), (8, s4), (10, s5), (12, s6)]

            for (tt_, rr_) in zip(QT.t + Et.t, QT.r + Et.r):
                P.add("pool", i_memset(tt_[:], 0.0), writes=[rr_])
                rr_.readers.append(P.all[-1])
            prep_wdn()
            assert NKC >= 14
            gen_D(0, range(NKC))
            for i in range(min(2, NI)):
                at_load(i)
            for g in range(min(LAG, NGI)):
                qk(g)
            pend = []
            per_item_D = (NKC + H - 1) // H
            for g in range(NGI):
                i, j = G[g]
                qi, h = items[i]
                q0, w = aq[qi]
                k_t, k_r, v_t, v_r, q_t, q_r = ld[i]
                t0, nj = cfg.tiles[j]
                cneg = -8.0 * cfg.slopes[h]
                b = qi % 2
                s = i % 2
                if j == 0:
                    if i + 2 < NI:
                        at_load(i + 2)
                    if qi + 1 < NAQ:
                        gen_D(qi + 1, range(h * per_item_D, min(NKC, (h + 1) * per_item_D)))
                s_t, s_r = sdict.pop(g)
                m_t, m_r = tmp.next()
                e_t, e_r = Et.next()
                dbc = Dt[b][0:nj, j, 0:w].unsqueeze(1).to_broadcast([nj, 2, w])
                P.add("dve", i_stt(v2(m_t, 0, nj, w), dbc, cneg, v2(s_t, 0, nj, w), ALU.mult, ALU.add),
                      reads=[D_r[b][j], s_r], writes=[m_r])
                P.add("act", i_act(v2(e_t, 0, nj, w), v2(m_t, 0, nj, w), AF.Exp, scale=0.125),
                      reads=[m_r], writes=[e_r])
                if g + LAG < NGI:
                    qk(g + LAG)
                P.add("pe", i_mm(accN[s][:, 0:AW + w], v_t[0:nj, j, :], e_t[0:nj, 0:AW + w], j == 0, j == NKC - 1),
                      reads=[e_r, v_r], **acc_rw(accN_r[s], j == 0))
                P.add("pe", i_mm(accD[s][:, 0:AW + w], onesb[0:nj, :], e_t[0:nj, 0:AW + w], j == 0, j == NKC - 1),
                      reads=[e_r, onesb_r], **acc_rw(accD_r[s], j == 0))
                while pend and pend[0][0] <= g:
                    pend.pop(0)[1]()
                if j == NKC - 1:
                    for (dl, fn) in epilogue_stages(i):
                        pend.append((g + dl, fn))
                    pend.sort(key=lambda x: x[0])
                    ld.pop(i)
            while pend:
                pend.pop(0)[1]()
            P.emit_phase("att")

        JB = 4
        nrm = 1.0 / math.sqrt(cfg.L * GD)
        with ExitStack() as st:
            G1r = Ring(nc, st, "fo_G1", [128, NKC, GD], BF16, 2)
            G2r = Ring(nc, st, "fo_G2", [128, NKC, GD], BF16, 2)
            slab = Ring(nc, st, "fo_slab", [128, JB, 2, QW], BF16, 4)
            fw0, fw0_r = sb(nc, st, "fo_fw0", [128, NG * GC], F32)
            fw, fw_r = sb(nc, st, "fo_fw", [128, NG * GC], F32)
            onesb, onesb_r = sb(nc, st, "fo_ones", [128, 128], BF16)
            acc = Ring(nc, st, "fo_acc", [128, 512], F32, 8, psum=True)
            sq = Ring(nc, st, "fo_sq", [128, QW], BF16, 4)
            rstd = Ring(nc, st, "fo_rstd", [128, QW], F32, 2)
            ostg = Ring(nc, st, "fo_o", [128, QW], BF16, 4)
            lnr = Ring(nc, st, "fo_ln", [128, QW], F32, 2)
            epsc, epsc_r = sb(nc, st, "fo_eps", [128, 1], F32)
            P.add("dve", i_memset(epsc[:], NORM_EPS), writes=[epsc_r])
            DMA(fw0[:], fw_d[:, :], writes=[fw0_r], key=fw0_r)
            P.add("pool", i_memset(onesb[:], 1.0), writes=[onesb_r])
            P.add("dve", i_ts(fw[:, :], fw0[:, :], nrm, None, ALU.mult), reads=[fw0_r], writes=[fw_r])
            sq_scale = nrm / math.sqrt(GD)
            slabs = list(range(0, NKC, JB))
            items = [(g, qi, si) for g in range(NG) for qi in range(NQC) for si in range(len(slabs))]
            gl = {}
            sl = {}
            fstate = {}

            def fo_load_g(g):
                g1 = G1r.next()
                g2 = G2r.next()
                gl[g] = (g1, g2)
                for cs, (gt, gr) in enumerate((g1, g2)):
                    DMA(gt[:], G_s[cs, g],
                        reads=[G_res[g]], writes=[gr], key=gr)

            def fo_load(i):
                g, qi, si = items[i]
                j0 = slabs[si]
                jn = min(JB, NKC - j0)
                sl_t, sl_r = slab.next()
                sl[i] = (sl_t, sl_r)
                DMA(sl_t[:, 0:jn, :, :],
                    dft_d[qi, j0 * 128:(j0 + jn) * 128, :, :].rearrange("(j p) s q -> p j s q", p=128),
                    writes=[sl_r], key=sl_r)

            def fo_compute(i):
                g, qi, si = items[i]
                q0, w = cfg.qchunks[qi]
                j0 = slabs[si]
                jn = min(JB, NKC - j0)
                sl_t, sl_r = sl.pop(i)
                if qi == 0 and si == 0 and g + 1 < NG:
                    fo_load_g(g + 1)
                if si == 0:
                    fstate["a"] = [acc.next() for _ in range(GC)]
                a = fstate["a"]
                gpair = gl[g]
                for jj in range(jn):
                    j = j0 + jj
                    nj = cfg.tiles[j][1]
                    for cs in range(2):
                        gt, gr = gpair[cs]
                        first = (j == 0 and cs == 0)
                        lastm = (j == NKC - 1 and cs == 1)
                        for cc in range(GC):
                            P.add("pe", i_mm(a[cc][0][:, 0:w], gt[0:nj, j, cc * 128:(cc + 1) * 128],
                                             sl_t[0:nj, jj, cs, 0:w], first, lastm),
                                  reads=[gr, sl_r], **acc_rw(a[cc][1], first))
                if si != len(slabs) - 1:
                    return
                ss_t, ss_r = acc.next()
                sqs = []
                for cc in range(GC):
                    sq_t, sq_r = sq.next()
                    sqs.append((sq_t, sq_r))
                    P.add("act", i_act(sq_t[:, 0:w], a[cc][0][:, 0:w], AF.Square, scale=sq_scale),
                          reads=[a[cc][1]], writes=[sq_r])
                for cc in range(GC):
                    P.add("pe", i_mm(ss_t[:, 0:w], onesb[:, :], sqs[cc][0][:, 0:w], cc == 0, cc == GC - 1),
                          reads=[sqs[cc][1], onesb_r], **acc_rw(ss_r, cc == 0))
                rs_t, rs_r = rstd.next()
                ln_t, ln_r = lnr.next()
                P.add("act", i_act(ln_t[:, 0:w], ss_t[:, 0:w], AF.Ln, bias=epsc[:, 0:1]), reads=[ss_r, epsc_r],
                      writes=[ln_r])
                P.add("act", i_act(rs_t[:, 0:w], ln_t[:, 0:w], AF.Exp, scale=-0.5), reads=[ln_r], writes=[rs_r])
                for cc in range(GC):
                    o_t, o_r = ostg.next()
                    col = g * GC + cc
                    P.add("dve", i_stt(o_t[:, 0:w], a[cc][0][:, 0:w], fw[:, col:col + 1], rs_t[:, 0:w],
                                       ALU.mult, ALU.mult), reads=[a[cc][1], rs_r, fw_r], writes=[o_r])
                    r0 = ATT + g * GD + cc * 128
                    DMA(mixT_s[r0:r0 + 128, q0:q0 + w], o_t[:, 0:w], reads=[o_r], pwrites=[mixF_res[g]], key=o_r)

            fo_load_g(0)
            pipeline(len(items), fo_load, fo_compute, 3)
            P.emit_phase("four")

        own_tiles = cfg.tiles[:cfg.n_own_tiles]
        wgroups, cur = [], []
        for (t0, n) in own_tiles:
            if cur and (t0 + n - cur[0][0]) > 528:
                wgroups.append(cur)
                cur = []
            cur.append((t0, n))
        if cur:
            wgroups.append(cur)
        mix_all = mixA_res + mixF_res
        with ExitStack() as st:
            mixT = Ring(nc, st, "wo_mix", [128, KD, 528], BF16, 2)
            wblk = Ring(nc, st, "wo_w", [128, KD, 512], BF16, 2)
            xp = Ring(nc, st, "wo_x", [128, 512], F32, 12)
            hp = Ring(nc, st, "wo_h", [128, 512], F32, 4)
            acc = Ring(nc, st, "wo_acc", [128, 512], F32, 8, psum=True)
            items = [(gi, nb) for gi in range(len(wgroups)) for nb in range(NB)]
            ms = {}
            ws = {}
            xs = {}

            def wo_load_mix(gi):
                grp = wgroups[gi]
                g0 = grp[0][0]
                gn = grp[-1][0] + grp[-1][1] - g0
                m_t, m_r = mixT.next()
                ms[gi] = (m_t, m_r)
                DMA(m_t[:, :, 0:gn], kp(mixT_s[:, g0:g0 + gn]), reads=mix_all, writes=[m_r], key=m_r)

            def wo_load(i):
                gi, nb = items[i]
                w_t, w_r = wblk.next()
                ws[i] = (w_t, w_r)
                DMA(w_t[:], woutb[nb], reads=[wout_res[nb]], writes=[w_r], key=w_r)
                xl = []
                for (t0, n) in wgroups[gi]:
                    x_t, x_r = xp.next()
                    xl.append((x_t, x_r))
                    DMA(x_t[0:n, :], xall[t0:t0 + n, nb * 512:(nb + 1) * 512], writes=[x_r], key=x_r)
                xs[i] = xl

            def wo_compute(i):
                gi, nb = items[i]
                grp = wgroups[gi]
                g0 = grp[0][0]
                if nb == 0 and gi + 1 < len(wgroups):
                    wo_load_mix(gi + 1)
                m_t, m_r = ms[gi]
                w_t, w_r = ws.pop(i)
                xl = xs.pop(i)
                for ti, (t0, n) in enumerate(grp):
                    a_t, a_r = acc.next()
                    x_t, x_r = xl[ti]
                    h_t, h_r = hp.next()
                    for k in range(KD):
                        P.add("pe", i_mm(a_t[0:n, :], m_t[:, k, t0 - g0:t0 - g0 + n], w_t[:, k, :], k == 0, k == KD - 1),
                              reads=[m_r, w_r], **acc_rw(a_r, k == 0))
                    P.add("dve", i_tt(h_t[0:n, :], a_t[0:n, :], x_t[0:n, :], ALU.add), reads=[a_r, x_r], writes=[h_r])
                    DMA(h2_s[t0:t0 + n, nb * 512:(nb + 1) * 512], h_t[0:n, :], reads=[h_r], pwrites=[h2_res], key=h_r)

            wo_load_mix(0)
            pipeline(len(items), wo_load, wo_compute, 1)
            P.emit_phase("wout")

        norm_phase("n2", h2_s, own_tiles, n2w_d, NORM_EPS, "T", dstT=hn2T_s, dst_res=hn2T_res, src_res=h2_res)

        fgroups = split(HALF, 512)
        TMAX = max(w for _, w in fgroups)
        KS = 4
        with ExitStack() as st:
            hn2T, hn2T_r = sb(nc, st, "ff_hn", [128, KD, TMAX + 2], BF16)
            actT = st.enter_context(nc.sbuf_tensor("ff_act", [128, NFF, TMAX], BF16))
            act_r = Res("ff_act")
            wup = Ring(nc, st, "ff_wup", [128, KD, 256], BF16, 2)
            wdn = Ring(nc, st, "ff_wdn", [128, KS, 512], BF16, 4)
            cw, cw_r = sb(nc, st, "ff_cw", [128, NFF, 3], F32)
            cbv, cb_r = sb(nc, st, "ff_cb", [128, NFF], F32)
            gm, gm_r = sb(nc, st, "ff_gm", [128, 1], F32)
            gs = Ring(nc, st, "ff_gs", [128, TMAX + 2], F32, 2)
            gh = Ring(nc, st, "ff_gh", [128, 1], F32, 2)
            cA = Ring(nc, st, "ff_cA", [128, TMAX], F32, 2)
            cB = Ring(nc, st, "ff_cB", [128, TMAX], F32, 2)
            hp = Ring(nc, st, "ff_hp", [128, 512], F32, 6)
            op = Ring(nc, st, "ff_op", [128, 512], F32, 3)
            acc = Ring(nc, st, "ff_acc", [128, 512], F32, 8, psum=True)
            DMA(cw[:], cw_d[:, :, :], writes=[cw_r], key=cw_r)
            DMA(cbv[:], cb_d[:, :], writes=[cb_r], key=cb_r)
            DMA(gm[:], mask_d[:, :], writes=[gm_r], key=gm_r)
            for gi, (r0, T) in enumerate(fgroups):
                lastg = gi == len(fgroups) - 1
                DMA(hn2T[:, :, 0:T + 2], kp(hn2T_s[:, r0:r0 + T + 2]), reads=[hn2T_res], writes=[hn2T_r], key=hn2T_r)
                gch = split(T + 2, 512)
                ws = {}

                def up_load(j):
                    w_t, w_r = wup.next()
                    ws[j] = (w_t, w_r)
                    DMA(w_t[:], wupb[j], reads=[wup_res[j]], writes=[w_r], key=w_r)

                def up_compute(j, T=T, gch=gch, lastg=lastg):
                    w_t, w_r = ws.pop(j)
                    gacc = []
                    for (c0, cwid) in gch:
                        a_t, a_r = acc.next()
                        gacc.append((a_t, a_r, c0, cwid))
                        for k in range(KD):
                            P.add("pe", i_mm(a_t[:, 0:cwid], w_t[:, k, 0:128], hn2T[:, k, c0:c0 + cwid], k == 0, k == KD - 1),
                                  reads=[w_r, hn2T_r], **acc_rw(a_r, k == 0))
                    v_t, v_r = acc.next()
                    for k in range(KD):
                        P.add("pe", i_mm(v_t[:, 0:T], w_t[:, k, 128:256], hn2T[:, k, 1:1 + T], k == 0, k == KD - 1),
                              reads=[w_r, hn2T_r], **acc_rw(v_r, k == 0))
                    g_t, g_r = gs.next()
                    for idx, (a_t, a_r, c0, cwid) in enumerate(gacc):
                        P.add("act", i_act(g_t[:, c0:c0 + cwid], a_t[:, 0:cwid], AF.Copy), reads=[a_r],
                              **acc_rw(g_r, idx == 0))
                    A_t, A_r = cA.next()
                    B_t, B_r = cB.next()
                    P.add("dve", i_ts(A_t[:, 0:T], g_t[:, 1:T + 1], cw[:, j, 1:2], cbv[:, j:j + 1], ALU.mult, ALU.add),
                          reads=[g_r, cw_r, cb_r], writes=[A_r])
                    P.add("dve", i_stt(B_t[:, 0:T], g_t[:, 0:T], cw[:, j, 0:1], A_t[:, 0:T], ALU.mult, ALU.add),
                          reads=[g_r, cw_r, A_r], writes=[B_r])
                    if lastg:
                        gh_t, gh_r = gh.next()
                        P.add("dve", i_ts(gh_t[:, 0:1], g_t[:, T + 1:T + 2], gm[:, 0:1], None, ALU.mult),
                              reads=[g_r, gm_r], writes=[gh_r])
                        P.add("dve", i_stt(A_t[:, 0:T - 1], g_t[:, 2:T + 1], cw[:, j, 2:3], B_t[:, 0:T - 1],
                                           ALU.mult, ALU.add), reads=[g_r, cw_r, B_r], writes=[A_r])
                        P.add("dve", i_stt(A_t[:, T - 1:T], gh_t[:, 0:1], cw[:, j, 2:3], B_t[:, T - 1:T],
                                           ALU.mult, ALU.add), reads=[gh_r, cw_r, B_r], pwrites=[A_r])
                    else:
                        P.add("dve", i_stt(A_t[:, 0:T], g_t[:, 2:T + 2], cw[:, j, 2:3], B_t[:, 0:T], ALU.mult, ALU.add),
                              reads=[g_r, cw_r, B_r], writes=[A_r])
                    P.add("act", i_act(B_t[:, 0:T], A_t[:, 0:T], AF.Gelu), reads=[A_r], writes=[B_r])
                    P.add("dve", i_tt(actT[:, j, 0:T], v_t[:, 0:T], B_t[:, 0:T], ALU.mult),
                          reads=[B_r, v_r], **acc_rw(act_r, j == 0))

                pipeline(NFF, up_load, up_compute, 1)

                ttiles = [(o, min(128, T - o)) for o in range(0, T, 128)]
                fslabs = list(range(0, NFF, KS))
                ditems = [(nb, si) for nb in range(NB) for si in range(len(fslabs))]
                ds = {}
                hs = {}
                dstate = {}

                def dn_load(i, ditems=ditems, fslabs=fslabs, ttiles=ttiles, r0=r0):
                    nb, si = ditems[i]
                    f0 = fslabs[si]
                    fn = min(KS, NFF - f0)
                    d_t, d_r = wdn.next()
                    ds[i] = (d_t, d_r)
                    DMA(d_t[:, 0:fn, :], wdnb[nb, :, f0:f0 + fn, :], reads=[wdn_res[nb]], writes=[d_r], key=d_r)

                def dn_compute(i, ditems=ditems, fslabs=fslabs, ttiles=ttiles, r0=r0):
                    nb, si = ditems[i]
                    f0 = fslabs[si]
                    fn = min(KS, NFF - f0)
                    d_t, d_r = ds.pop(i)
                    if si == max(0, len(fslabs) - 5):
                        hl = []
                        for (o, n) in ttiles:
                            h_t, h_r = hp.next()
                            hl.append((h_t, h_r))
                            row = r0 + o
                            DMA(h_t[0:n, :], h2_s[row + 1:row + 1 + n, nb * 512:(nb + 1) * 512], reads=[h2_res],
                                writes=[h_r], key=h_r)
                        hs[nb] = hl
                    if si == 0:
                        dstate["acc"] = [acc.next() for _ in ttiles]
                    accs_nb = dstate["acc"]
                    for ff in range(fn):
                        f = f0 + ff
                        for ti, (o, n) in enumerate(ttiles):
                            a_t, a_r = accs_nb[ti]
                            P.add("pe", i_mm(a_t[0:n, :], actT[:, f, o:o + n], d_t[:, ff, :], f == 0, f == NFF - 1),
                                  reads=[act_r, d_r], **acc_rw(a_r, f == 0))
                    if si != len(fslabs) - 1:
                        return
                    hl = hs.pop(nb)
                    for ti, (o, n) in enumerate(ttiles):
                        a_t, a_r = accs_nb[ti]
                        h_t, h_r = hl[ti]
                        o_t, o_r = op.next()
                        row = r0 + o
                        P.add("dve", i_tt(o_t[0:n, :], a_t[0:n, :], h_t[0:n, :], ALU.add), reads=[a_r, h_r], writes=[o_r])
                        DMA(y[row:row + n, nb * 512:(nb + 1) * 512], o_t[0:n, :], reads=[o_r], pwrites=[y_res], key=o_r)

                pipeline(len(ditems), dn_load, dn_compute, 3)
            P.emit_phase("ffn")

        ytiles = [(o, min(128, HALF - o)) for o in range(0, HALF, 128)]
        norm_phase("fin", y, ytiles, nfw_d, NORM_EPS, "F", dst_res=y_res, src_res=y_res, dst_rows=y)

        fin = list(P.final_events) + [(s, s.count) for s in P.persist_sems]
        with nc.Block() as block:
            def tail(eng):
                for (sem, val) in fin:
                    eng.wait_ge(sem.h, val)
            block.sync(tail)
    return nc


def prepare_inputs(cfg, x, meta_tokens, norm1_w, w_in, lambda_q1, lambda_k1, lambda_q2, lambda_k2,
                   diff_subln_w, fourier_norm_w, w_out, norm2_w, w_up, conv_w, conv_b, w_down, final_norm_w):
    f32 = np.float32
    bf16 = ml_dtypes.bfloat16
    D, L, NOWN, NTOK, HALF, NMETA, SEQ = cfg.D, cfg.L, cfg.NOWN, cfg.NTOK, cfg.HALF, cfg.NMETA, cfg.SEQ
    NKC, QW, GD, GC, NFF = cfg.NKC, cfg.QW, cfg.GD, cfg.GC, cfg.NFF
    x = np.asarray(x, f32)
    meta = np.asarray(meta_tokens, f32)
    w_in2 = np.ascontiguousarray(np.asarray(w_in, f32)[0])
    w_out2 = np.ascontiguousarray(np.asarray(w_out, f32)[0])
    w_up2 = np.ascontiguousarray(np.asarray(w_up, f32)[0])
    w_down2 = np.ascontiguousarray(np.asarray(w_down, f32)[0])
    bc = lambda v: np.ascontiguousarray(np.broadcast_to(np.asarray(v, f32).reshape(1, -1), (128, np.asarray(v).size)))
    n1w, n2w, nfw = bc(norm1_w[0]), bc(norm2_w[0]), bc(final_norm_w)
    lamv = np.ascontiguousarray(np.broadcast_to(
        np.stack([np.asarray(v, f32)[0] for v in (lambda_q1, lambda_k1, lambda_q2, lambda_k2)])[None], (128, 4, 64)))
    subw = np.ascontiguousarray(np.asarray(diff_subln_w, f32)[0].reshape(128, 1))
    fw = np.ascontiguousarray(np.asarray(fourier_norm_w, f32)[0].reshape(cfg.NG * GC, 128).T)
    cw = np.ascontiguousarray(np.asarray(conv_w, f32)[0].reshape(3, NFF, 128).transpose(2, 1, 0))
    cb = np.ascontiguousarray(np.asarray(conv_b, f32)[0].reshape(NFF, 128).T)
    ident = np.eye(128, dtype=f32).astype(bf16)
    cc = np.arange(GD)
    ang = 2.0 * np.pi * ((cc[:, None] * cc[None, :]) % GD) / GD
    cd = np.stack([np.cos(ang), np.sin(ang)]).reshape(2, GC, 128, GD).transpose(0, 2, 1, 3)
    cdft = np.ascontiguousarray(cd).astype(bf16)

    PADPOS = 1.0e6
    per_t = {}
    for t in range(2):
        own_pos = np.arange(NOWN) + (NMETA - 1) + HALF * t
        if t == 0:
            oth_pos = np.concatenate([np.arange(0, NMETA - 1), np.arange(NMETA + HALF + 1, L), [L]])
        else:
            oth_pos = np.concatenate([np.arange(0, NMETA), np.arange(NMETA, NMETA + HALF - 1)])
        assert len(oth_pos) == cfg.NOTH, (len(oth_pos), cfg.NOTH)
        allpos = np.concatenate([own_pos, oth_pos])
        assert len(allpos) == NTOK and len(np.unique(allpos)) == NTOK
        ispad = allpos >= L
        posq = own_pos.astype(f32)
        posk_flat = np.where(ispad, PADPOS, allpos).astype(f32)
        posk = np.zeros((128, NKC), f32)
        for j, (t0, n) in enumerate(cfg.tiles):
            posk[:n, j] = posk_flat[t0:t0 + n]
        dft = np.zeros((len(cfg.qchunks), NKC * 128, 2, QW), f32)
        pl = allpos.astype(np.int64)
        po = own_pos.astype(np.int64)
        for qi, (q0, w) in enumerate(cfg.qchunks):
            for j, (t0, n) in enumerate(cfg.tiles):
                prod = (pl[t0:t0 + n, None] * po[None, q0:q0 + w]) % L
                a = 2.0 * np.pi * prod / L
                valid = (~ispad[t0:t0 + n])[:, None] & (po[None, q0:q0 + w] < L)
                dft[qi, j * 128:j * 128 + n, 0, :w] = np.where(valid, np.cos(a), 0.0)
                dft[qi, j * 128:j * 128 + n, 1, :w] = np.where(valid, -np.sin(a), 0.0)
        per_t[t] = dict(idx=allpos, posq=np.ascontiguousarray(np.broadcast_to(posq[None], (128, NOWN))),
                        posk=posk, dft=dft.astype(bf16),
                        gmask=np.full((128, 1), 1.0 if t == 0 else 0.0, f32))
    in_maps = []
    for c in range(2 * cfg.B):
        b, t = c // 2, c % 2
        pt = per_t[t]
        table = np.concatenate([meta, x[b], np.zeros((1, D), f32)], axis=0)
        xall = table[np.minimum(pt["idx"], L)]
        in_maps.append(dict(
            xall=xall, posq=pt["posq"], posk=pt["posk"], dft=pt["dft"], cdft=cdft,
            w_in=w_in2, w_out=w_out2, w_up=w_up2, w_down=w_down2, n1w=n1w, n2w=n2w, nfw=nfw,
            lamv=lamv, subw=subw, fw=fw, cw=cw, cb=cb, gmask=pt["gmask"], ident=ident))
    return in_maps


def run(cfg, inputs, trace=False):
    nc = build_program(cfg)
    in_maps = prepare_inputs(cfg, **inputs)
    n = 2 * cfg.B
    res = run_bass_kernel_spmd(nc, in_maps, core_ids=list(range(n)), trace=trace)
    out = np.zeros((cfg.B, cfg.SEQ, cfg.D), np.float32)
    for c in range(n):
        b, t = c // 2, c % 2
        out[b, t * cfg.HALF:(t + 1) * cfg.HALF] = res.results[c]["y"]
    return out, res


def kernel(**inputs):
    cfg = Cfg()
    out, _ = run(cfg, inputs)
    return out
```

```python
import math
from contextlib import ExitStack

import numpy as np
import ml_dtypes

import concourse.bass as bass
import concourse.mybir as mybir
from concourse.bass_utils import run_bass_kernel_spmd

F32 = mybir.dt.float32
BF16 = mybir.dt.bfloat16
AF = mybir.ActivationFunctionType
ALU = mybir.AluOpType
AX = mybir.AxisListType

NORM_EPS = 1e-6
SUBLN_EPS = 1e-5


class Cfg:
    def __init__(s, D=4096, SEQ=4096, B=4, NMETA=16, H=16, NG=4, GD=512, DFF=11008):
        s.D, s.SEQ, s.B, s.NMETA, s.H, s.NG, s.GD, s.DFF = D, SEQ, B, NMETA, H, NG, GD, DFF
        s.QK = H * 128
        s.ATT = H * 128
        s.FW = NG * GD
        s.IN = 2 * s.QK + s.ATT + s.FW
        s.MIX = s.ATT + s.FW
        assert s.MIX == D
        s.L = NMETA + SEQ
        s.HALF = SEQ // 2
        s.NOWN = s.HALF + 2
        s.NTOK = s.L + 1
        s.NOTH = s.NTOK - s.NOWN
        s.KD = D // 128
        s.NFF = DFF // 128
        s.GC = GD // 128
        assert D % 512 == 0 and DFF % 128 == 0 and s.IN % 256 == 0 and GD % 128 == 0
        s.tiles = []
        for base, n in ((0, s.NOWN), (s.NOWN, s.NOTH)):
            o = 0
            while o < n:
                m = min(128, n - o)
                s.tiles.append((base + o, m))
                o += m
        s.NKC = len(s.tiles)
        s.n_own_tiles = sum(1 for (t0, n) in s.tiles if t0 < s.NOWN)
        s.qchunks = split(s.NOWN, 512)
        s.QW = max(w for _, w in s.qchunks)
        s.lam_init = 0.8 - 0.6 * math.exp(-0.3 * 0)

        s.slopes = [2.0 ** (-8.0 * (h + 1) / H) for h in range(H)]

    def tile_pos_ranges(s):
        out = []
        for t in range(2):
            own_pos, oth_pos = _positions(s, t)
            allpos = np.concatenate([own_pos, oth_pos])
            r = []
            for (t0, n) in s.tiles:
                p = allpos[t0:t0 + n]
                p = p[p < s.L]
                r.append((int(p.min()), int(p.max())) if len(p) else (10 ** 9, 10 ** 9))
            out.append(r)
        return out


def _positions(cfg, t):
    NMETA, HALF, L, NOWN = cfg.NMETA, cfg.HALF, cfg.L, cfg.NOWN
    own_pos = np.arange(NOWN) + (NMETA - 1) + HALF * t
    if t == 0:
        oth_pos = np.concatenate([np.arange(0, NMETA - 1), np.arange(NMETA + HALF + 1, L), [L]])
    else:
        oth_pos = np.concatenate([np.arange(0, NMETA), np.arange(NMETA, NMETA + HALF - 1)])
    return own_pos, oth_pos


def split(n, mx):
    k = (n + mx - 1) // mx
    base, rem = divmod(n, k)
    out, o = [], 0
    for i in range(k):
        w = base + (1 if i < rem else 0)
        out.append((o, w))
        o += w
    return out


class Res:
    __slots__ = ("name", "writers", "readers", "sem")

    def __init__(self, name):
        self.name = name
        self.writers = []
        self.readers = []
        self.sem = None


class Sem:
    __slots__ = ("h", "count")

    def __init__(self, h):
        self.h = h
        self.count = 0


class Task:
    __slots__ = ("eng", "fn", "deps", "signal", "sem", "val", "is_dma", "idx")


COMPUTE = ("pe", "act", "dve", "pool")
ENGS = ("pe", "act", "dve", "pool", "sp")


class Prog:
    def __init__(self, nc, gstack):
        self.nc = nc
        self.gstack = gstack
        self.nsem = 0
        self.eng_sem = {e: self._new_sem(e) for e in COMPUTE}
        self.dma_pool = []
        self.dma_pool_used = 0
        self.tasks = {e: [] for e in ENGS}
        self.all = []
        self.barrier_events = []
        self.phase_dma_sems = []
        self.persist_sems = []

    def _new_sem(self, name):
        self.nsem += 1
        h = self.gstack.enter_context(self.nc.semaphore(f"s{self.nsem}_{name}"))
        return Sem(h)

    def dma_sem(self):
        if self.dma_pool_used < len(self.dma_pool):
            s = self.dma_pool[self.dma_pool_used]
        else:
            s = self._new_sem("dma")
            self.dma_pool.append(s)
        self.dma_pool_used += 1
        return s

    def persistent_sem(self):
        s = self._new_sem("pdma")
        self.persist_sems.append(s)
        return s

    def add(self, eng, fn, reads=(), writes=(), pwrites=(), key=None, qeng=None):
        t = Task()
        t.is_dma = eng == "dma"
        t.eng = qeng if t.is_dma else eng
        t.fn = fn
        deps = []
        for r in reads:
            deps.extend(r.writers)
        for w in writes:
            deps.extend(w.writers)
            deps.extend(w.readers)
        for w in pwrites:
            deps.extend(w.readers)
        seen = set()
        t.deps = []
        latest = {}
        for d in deps:
            if id(d) in seen:
                continue
            seen.add(id(d))
            if (not d.is_dma) and (not t.is_dma) and d.eng == "pe" and t.eng == "pe":
                continue
            if d.is_dma:
                t.deps.append(d)
                d.signal = True
            else:
                if d.eng not in latest or d.idx > latest[d.eng].idx:
                    latest[d.eng] = d
        for d in latest.values():
            t.deps.append(d)
            d.signal = True
        t.signal = t.is_dma
        if t.is_dma:
            if key.sem is None:
                key.sem = self.dma_sem()
                self.phase_dma_sems.append(key.sem)
            t.sem = key.sem
            key.sem.count += 16
            t.val = key.sem.count
        else:
            t.sem = None
            t.val = None
        for w in writes:
            w.writers = [t]
            w.readers = []
        for w in pwrites:
            w.writers.append(t)
        for r in reads:
            r.readers.append(t)
        t.idx = len(self.all)
        self.tasks[t.eng].append(t)
        self.all.append(t)
        return t

    def emit_phase(self, name):
        nc = self.nc
        for e in COMPUTE:
            if self.tasks[e]:
                self.tasks[e][-1].signal = True
        for e in COMPUTE:
            s = self.eng_sem[e]
            for t in self.tasks[e]:
                if t.is_dma:
                    continue
                if t.signal:
                    if s.count >= 30000:
                        s = self._new_sem(e)
                        self.eng_sem[e] = s
                    s.count += 1
                    t.sem = s
                    t.val = s.count
        barrier = list(self.barrier_events)

        def run(engname):
            def body(eng):
                waited = {}

                def wait(sem, val):
                    if waited.get(id(sem), 0) >= val:
                        return
                    waited[id(sem)] = val
                    eng.wait_ge(sem.h, val)

                for (sem, val) in barrier:
                    wait(sem, val)
                for t in self.tasks[engname]:
                    need = {}
                    for d in t.deps:
                        if id(d.sem) not in need or need[id(d.sem)][1] < d.val:
                            need[id(d.sem)] = (d.sem, d.val)
                    for (sem_, val_) in need.values():
                        wait(sem_, val_)
                    ins = t.fn(eng)
                    if t.signal:
                        ins.then_inc(t.sem.h, 16 if t.is_dma else 1)
            return body

        with nc.Block() as block:
            block.tensor(run("pe"))
            block.scalar(run("act"))
            block.vector(run("dve"))
            block.gpsimd(run("pool"))
            block.sync(run("sp"))
        ev = []
        for e in COMPUTE:
            if self.tasks[e]:
                lt = [t for t in self.tasks[e] if not t.is_dma]
                if lt:
                    ev.append((lt[-1].sem, lt[-1].val))
        for s in self.phase_dma_sems:
            ev.append((s, s.count))
        self.barrier_events = ev
        self.final_events = ev
        self.tasks = {e: [] for e in ENGS}
        self.phase_dma_sems = []
        self.dma_pool_used = 0

    def forget(self, *ress):
        for r in ress:
            r.writers = []
            r.readers = []


class Ring:
    def __init__(self, nc, stack, name, shape, dtype, n, psum=False):
        self.t = []
        self.r = []
        for i in range(n):
            if psum:
                self.t.append(stack.enter_context(nc.psum_tensor(f"{name}{i}", shape, dtype)))
            else:
                self.t.append(stack.enter_context(nc.sbuf_tensor(f"{name}{i}", shape, dtype)))
            self.r.append(Res(f"{name}{i}"))
        self.i = 0
        self.n = n

    def next(self):
        k = self.i % self.n
        self.i += 1
        r = self.r[k]
        if r.writers and not r.readers:
            raise RuntimeError(f"ring {r.name} oversubscribed: slot handed out again before its data was read")
        return self.t[k], r


def sb(nc, stack, name, shape, dtype):
    return stack.enter_context(nc.sbuf_tensor(name, shape, dtype)), Res(name)


def i_dma(out, in_):
    return lambda e: e.dma_start(out=out, in_=in_)


def i_mm(out, lhsT, rhs, start, stop):
    return lambda e: e.matmul(out=out, lhsT=lhsT, rhs=rhs, start=start, stop=stop)


def i_tr(out, in_, ident):
    return lambda e: e.transpose(out=out, in_=in_, identity=ident)


def i_act(out, in_, func, **kw):
    return lambda e: e.activation(out=out, in_=in_, func=func, **kw)


def i_ts(out, in0, s1, s2, op0, op1=None):
    if op1 is None:
        return lambda e: e.tensor_scalar(out=out, in0=in0, scalar1=s1, scalar2=None, op0=op0)
    return lambda e: e.tensor_scalar(out=out, in0=in0, scalar1=s1, scalar2=s2, op0=op0, op1=op1)


def i_stt(out, in0, scalar, in1, op0, op1):
    return lambda e: e.scalar_tensor_tensor(out=out, in0=in0, scalar=scalar, in1=in1, op0=op0, op1=op1)


def i_tt(out, in0, in1, op):
    return lambda e: e.tensor_tensor(out=out, in0=in0, in1=in1, op=op)


def i_cp(out, in_):
    return lambda e: e.tensor_copy(out=out, in_=in_)


def i_recip(out, in_):
    return lambda e: e.reciprocal(out=out, in_=in_)


def i_memset(ap, v):
    return lambda e: e.memset(ap, v)


def i_reduce(out, in_, axis, op):
    return lambda e: e.tensor_reduce(out=out, in_=in_, axis=axis, op=op)


def build_program(cfg):
    nc = bass.Bass("TRN2", target_bir_lowering=False)
    D, KD, NTOK, NOWN, HALF = cfg.D, cfg.KD, cfg.NTOK, cfg.NOWN, cfg.HALF
    H, NG, GD, GC, NFF, NKC, QW = cfg.H, cfg.NG, cfg.GD, cfg.GC, cfg.NFF, cfg.NKC, cfg.QW
    IN, ATT, FW, DFF = cfg.IN, cfg.ATT, cfg.FW, cfg.DFF
    NQC = len(cfg.qchunks)
    NB = D // 512

    def din(name, shape, dt=F32):
        return nc.dram_tensor(name, list(shape), dt, kind="ExternalInput").ap()

    def dscr(name, shape, dt):
        return nc.dram_tensor(name, list(shape), dt, kind="Internal").ap()

    xall = din("xall", [NTOK, D])
    posq_d = din("posq", [128, NOWN])
    posk_d = din("posk", [128, NKC])
    dft_d = din("dft", [NQC, NKC * 128, 2, QW], BF16)
    cdft_d = din("cdft", [2, 128, GC, GD], BF16)
    w_in = din("w_in", [D, IN])
    w_out = din("w_out", [cfg.MIX, D])
    w_up = din("w_up", [D, 2 * DFF])
    w_down = din("w_down", [DFF, D])
    n1w_d = din("n1w", [128, D])
    n2w_d = din("n2w", [128, D])
    nfw_d = din("nfw", [128, D])
    lam_d = din("lamv", [128, 4, 64])
    subw_d = din("subw", [128, 1])
    fw_d = din("fw", [128, NG * GC])
    cw_d = din("cw", [128, NFF, 3])
    cb_d = din("cb", [128, NFF])
    mask_d = din("gmask", [128, 1])
    ident_d = din("ident", [128, 128], BF16)
    y = nc.dram_tensor("y", [HALF, D], F32, kind="ExternalOutput").ap()

    winb = dscr("winb", [IN // 256, 128, KD, 256], BF16)
    woutb = dscr("woutb", [NB, 128, KD, 512], BF16)
    wupb = dscr("wupb", [NFF, 128, KD, 256], BF16)
    wdnb = dscr("wdnb", [NB, 128, NFF, 512], BF16)
    hnT_s = dscr("hnT_s", [D, NTOK], BF16)
    KT_s = dscr("KT_s", [H, 128, NTOK], BF16)
    QT_s = dscr("QT_s", [H, 128, NOWN], BF16)
    V_s = dscr("V_s", [H, 128, NKC, 128], BF16)
    G_s = dscr("G_s", [2, NG, 128, NKC, GD], BF16)
    mixT_s = dscr("mixT_s", [cfg.MIX, NOWN], BF16)
    h2_s = dscr("h2_s", [NOWN, D], F32)
    hn2T_s = dscr("hn2T_s", [D, NOWN], BF16)

    def kp(ap):
        return ap.rearrange("(k p) t -> p k t", p=128)

    gstack = ExitStack()
    with gstack:
        P = Prog(nc, gstack)

        def DMA(out, in_, reads=(), writes=(), pwrites=(), key=None, q="sp"):
            return P.add("dma", i_dma(out, in_), reads=reads, writes=writes, pwrites=pwrites, key=key, qeng=q)

        def acc_rw(r, first):
            return dict(writes=[r], pwrites=[]) if first else dict(writes=[], pwrites=[r])

        nqb = cfg.QK // 256
        nvb = ATT // 256
        cb_order = (list(range(nqb, 2 * nqb)) + list(range(2 * nqb, 2 * nqb + nvb))
                    + list(range(2 * nqb + nvb, IN // 256)) + list(range(0, nqb)))
        KG = min(8, KD)
        win_res = {}
        wout_res = []
        wup_res = [None] * NFF
        wdn_res = []

        def prep_win(part):
            half = (len(cb_order) // 8) * 2
            rng = range(0, half, 2) if part == 0 else range(half, len(cb_order), 2)
            for gi in rng:
                r = Res(f"winb{gi}")
                r.sem = P.persistent_sem()
                for cb in cb_order[gi:gi + 2]:
                    win_res[cb] = r
                    for kg in range(0, KD, KG):
                        DMA(winb[cb, :, kg:kg + KG, :], kp(w_in[kg * 128:(kg + KG) * 128, cb * 256:(cb + 1) * 256]),
                            pwrites=[r], key=r, q="pool")

        def prep_wout():
            for nb in range(NB):
                r = Res(f"woutb{nb}")
                r.sem = P.persistent_sem()
                wout_res.append(r)
                for kg in range(0, KD, KG):
                    DMA(woutb[nb, :, kg:kg + KG, :], kp(w_out[kg * 128:(kg + KG) * 128, nb * 512:(nb + 1) * 512]),
                        pwrites=[r], key=r, q="pool")

        def prep_wup():
            JG = 16
            for j0 in range(0, NFF, JG):
                r = Res(f"wupb{j0}")
                r.sem = P.persistent_sem()
                for j in range(j0, min(NFF, j0 + JG)):
                    wup_res[j] = r
                    for kg in range(0, KD, KG):
                        for half, cbase in ((0, 0), (1, DFF)):
                            DMA(wupb[j, :, kg:kg + KG, half * 128:(half + 1) * 128],
                                kp(w_up[kg * 128:(kg + KG) * 128, cbase + j * 128:cbase + (j + 1) * 128]),
                                pwrites=[r], key=r, q="pool")

        def prep_wdn():
            FG = 8
            for nb in range(NB):
                r = Res(f"wdnb{nb}")
                r.sem = P.persistent_sem()
                wdn_res.append(r)
                for f0 in range(0, NFF, FG):
                    f1 = min(NFF, f0 + FG)
                    DMA(wdnb[nb, :, f0:f1, :], kp(w_down[f0 * 128:f1 * 128, nb * 512:(nb + 1) * 512]),
                        pwrites=[r], key=r, q="pool")

        def pipeline(n, load, compute, pf):
            for i in range(min(pf, n)):
                load(i)
            for i in range(n):
                if i + pf < n:
                    load(i + pf)
                compute(i)

        prep_win(0)

        hnT_res = Res("hnT_s")
        KT_res = [Res(f"KT{h}") for h in range(H)]
        QT_res = [Res(f"QT{h}") for h in range(H)]
        V_res = Res("V_s")
        G_res = [Res(f"G{g}") for g in range(NG)]
        mixA_res = [Res(f"mixA{h}") for h in range(H)]
        mixF_res = [Res(f"mixF{g}") for g in range(NG)]
        h2_res = Res("h2_s")
        hn2T_res = Res("hn2T_s")
        y_res = Res("y")

        evac_ctr = [0]

        def evac(out_ap, in_ap, reads, writes=(), pwrites=()):
            evac_ctr[0] += 1
            if evac_ctr[0] % 2:
                P.add("act", i_act(out_ap, in_ap, AF.Copy), reads=reads, writes=writes, pwrites=pwrites)
            else:
                P.add("dve", i_cp(out_ap, in_ap), reads=reads, writes=writes, pwrites=pwrites)

        def norm_phase(name, src, rows_tiles, wb_d, eps, mode, dstT=None, dst_res=None, src_res=None,
                       dst_rows=None, extra=None):
            with ExitStack() as st:
                wb, wb_r = sb(nc, st, name + "wb", [128, D], F32)
                xt = Ring(nc, st, name + "x", [128, D], F32, 4)
                ot = Ring(nc, st, name + "o", [128, D], BF16 if mode == "T" else F32, 2)
                ss = Ring(nc, st, name + "ss", [128, 1], F32, 4)
                rs = Ring(nc, st, name + "rs", [128, 1], F32, 4)
                ln = Ring(nc, st, name + "ln", [128, 1], F32, 4)
                epsc, epsc_r = sb(nc, st, name + "eps", [128, 1], F32)
                P.add("dve", i_memset(epsc[:], eps), writes=[epsc_r])
                DMA(wb[:], wb_d[:, :], writes=[wb_r], key=wb_r)
                if mode == "T":
                    ident, ident_r = sb(nc, st, name + "id", [128, 128], BF16)
                    tT = Ring(nc, st, name + "tT", [128, KD, 528], BF16, 2)
                    pst = Ring(nc, st, name + "ps", [128, 8, 128], BF16, 8, psum=True)
                    DMA(ident[:], ident_d[:, :], writes=[ident_r], key=ident_r)
                if extra is not None:
                    extra()
                groups, cur = [], []
                for (r0, n) in rows_tiles:
                    if cur and (cur[-1][0] + cur[-1][1] != r0 or (r0 + n - cur[0][0]) > 528):
                        groups.append(cur)
                        cur = []
                    cur.append((r0, n))
                if cur:
                    groups.append(cur)
                inv_sqrt_d = 1.0 / math.sqrt(D)
                sread = [src_res] if src_res is not None else []
                items = []
                for gi, grp in enumerate(groups):
                    for ti, (r0, n) in enumerate(grp):
                        items.append((r0, n, gi, ti == 0, ti == len(grp) - 1))
                xs = {}
                state = {}

                def load(i):
                    r0, n = items[i][0], items[i][1]
                    x_t, x_r = xt.next()
                    xs[i] = (x_t, x_r)
                    DMA(x_t[0:n, :], src[r0:r0 + n, :], reads=sread, writes=[x_r], key=x_r)

                def compute(i):
                    r0, n, gi, gfirst, glast = items[i]
                    grp = groups[gi]
                    g0 = grp[0][0]
                    x_t, x_r = xs.pop(i)
                    o_t, o_r = ot.next()
                    s_t, s_r = ss.next()
                    r_t, r_r = rs.next()
                    l_t, l_r = ln.next()
                    if mode == "T" and gfirst:
                        state["tT"] = tT.next()
                    P.add("act", i_act(o_t[0:n, :], x_t[0:n, :], AF.Square, scale=inv_sqrt_d,
                                       accum_out=s_t[0:n, :]), reads=[x_r], writes=[o_r, s_r])
                    P.add("act", i_act(l_t[0:n, :], s_t[0:n, :], AF.Ln, bias=epsc[0:n, 0:1]), reads=[s_r, epsc_r],
                          writes=[l_r])
                    P.add("act", i_act(r_t[0:n, :], l_t[0:n, :], AF.Exp, scale=-0.5), reads=[l_r], writes=[r_r])
                    P.add("dve", i_stt(o_t[0:n, :], x_t[0:n, :], r_t[0:n, 0:1], wb[0:n, :], ALU.mult, ALU.mult),
                          reads=[x_r, r_r, wb_r], writes=[o_r])
                    if mode == "F":
                        DMA(dst_rows[r0:r0 + n, :], o_t[0:n, :], reads=[o_r], pwrites=[dst_res], key=o_r)
                        return
                    tTt, tTr = state["tT"]
                    c0 = r0 - g0
                    for kb in range(0, KD, 8):
                        kn = min(8, KD - kb)
                        p_t, p_r = pst.next()
                        for kk in range(kn):
                            k = kb + kk
                            P.add("pe", i_tr(p_t[:, kk, 0:n], o_t[0:n, k * 128:(k + 1) * 128], ident[0:n, 0:n]),
                                  reads=[o_r, ident_r], **acc_rw(p_r, kk == 0))
                        evac(tTt[:, kb:kb + kn, c0:c0 + n], p_t[:, 0:kn, 0:n], reads=[p_r], pwrites=[tTr])
                    if glast:
                        gn = grp[-1][0] + grp[-1][1] - g0
                        DMA(kp(dstT[:, g0:g0 + gn]), tTt[:, :, 0:gn], reads=[tTr], pwrites=[dst_res], key=tTr)

                pipeline(len(items), load, compute, 3)
                P.emit_phase(name)

        norm_phase("n1", xall, cfg.tiles, n1w_d, NORM_EPS, "T", dstT=hnT_s, dst_res=hnT_res)

        supers, cur = [], []
        for (t0, n) in cfg.tiles:
            own = t0 < NOWN
            if cur and ((cur[0][0] < NOWN) != own or (t0 + n - cur[0][0]) > 1040):
                supers.append(cur)
                cur = []
            cur.append((t0, n))
        if cur:
            supers.append(cur)
        TS = max(s[-1][0] + s[-1][1] - s[0][0] for s in supers)
        tile_index = {t0: i for i, (t0, n) in enumerate(cfg.tiles)}
        with ExitStack() as st:
            hnTr = Ring(nc, st, "pj_hnT", [128, KD, TS], BF16, 2)
            wblk = Ring(nc, st, "pj_w", [128, KD, 256], BF16, 3)
            cdft, cdft_r = sb(nc, st, "pj_cdft", [128, 2, GC, GD], BF16)
            fT, fT_r = sb(nc, st, "pj_fT", [128, GC, TS], BF16)
            stg = Ring(nc, st, "pj_stg", [128, 512], BF16, 4)
            acc = Ring(nc, st, "pj_acc", [128, 512], F32, 8, psum=True)
            prep_win(1)
            prep_wout()
            prep_wup()
            for cs in range(2):
                DMA(cdft[:, cs, :, :], cdft_d[cs], pwrites=[cdft_r], key=cdft_r)
            zt, zt_r = sb(nc, st, "pj_zero", [128, max(ATT, FW)], BF16)
            P.add("dve", i_memset(zt[:], 0.0), writes=[zt_r])
            for ti, (t0, n) in enumerate(cfg.tiles):
                if n < 128:
                    DMA(V_s[:, n:128, ti, :].rearrange("h p c -> p h c"),
                        zt[n:128, 0:ATT].rearrange("p (h c) -> p h c", h=H), reads=[zt_r], pwrites=[V_res], key=zt_r)
                    for cs in range(2):
                        DMA(G_s[cs, :, n:128, ti, :].rearrange("g p c -> p g c"),
                            zt[n:128, 0:FW].rearrange("p (g c) -> p g c", g=NG), reads=[zt_r],
                            pwrites=G_res, key=zt_r)
            items = []
            for si, stiles in enumerate(supers):
                own = stiles[0][0] < NOWN
                blocks = [c for c in cb_order if own or c >= nqb]
                for bi, cb in enumerate(blocks):
                    items.append((si, cb, bi == 0))
            ws = {}
            hs = {}

            def pj_load(i):
                si, cb, sfirst = items[i]
                if sfirst:
                    stiles = supers[si]
                    s0 = stiles[0][0]
                    sn = stiles[-1][0] + stiles[-1][1] - s0
                    h_t, h_r = hnTr.next()
                    hs[si] = (h_t, h_r)
                    DMA(h_t[:, :, 0:sn], kp(hnT_s[:, s0:s0 + sn]), reads=[hnT_res], writes=[h_r], key=h_r)
                w_t, w_r = wblk.next()
                ws[i] = (w_t, w_r)
                DMA(w_t[:], winb[cb], reads=[win_res[cb]], writes=[w_r], key=w_r)

            def pj_compute(i):
                si, cb, sfirst = items[i]
                stiles = supers[si]
                s0 = stiles[0][0]
                sn = stiles[-1][0] + stiles[-1][1] - s0
                mch = split(sn, 512)
                hnT, hnT_r = hs[si]
                w_t, w_r = ws.pop(i)
                col0 = cb * 256
                if col0 < 2 * cfg.QK:
                    isq = col0 < cfg.QK
                    for hh in range(2):
                        h = (col0 - (0 if isq else cfg.QK)) // 128 + hh
                        for (m0, mw) in mch:
                            a_t, a_r = acc.next()
                            for k in range(KD):
                                P.add("pe", i_mm(a_t[:, 0:mw], w_t[:, k, hh * 128:(hh + 1) * 128],
                                                 hnT[:, k, m0:m0 + mw], k == 0, k == KD - 1),
                                      reads=[w_r, hnT_r], **acc_rw(a_r, k == 0))
                            g_t, g_r = stg.next()
                            evac(g_t[:, 0:mw], a_t[:, 0:mw], reads=[a_r], writes=[g_r])
                            if isq:
                                dst, dres = QT_s[h, :, s0 + m0:s0 + m0 + mw], QT_res[h]
                            else:
                                dst, dres = KT_s[h, :, s0 + m0:s0 + m0 + mw], KT_res[h]
                            DMA(dst, g_t[:, 0:mw], reads=[g_r], pwrites=[dres], key=g_r)
                elif col0 < 2 * cfg.QK + ATT:
                    vc0 = col0 - 2 * cfg.QK
                    for (t0, n) in stiles:
                        a_t, a_r = acc.next()
                        for k in range(KD):
                            P.add("pe", i_mm(a_t[0:n, 0:256], hnT[:, k, t0 - s0:t0 - s0 + n], w_t[:, k, :],
                                             k == 0, k == KD - 1), reads=[w_r, hnT_r], **acc_rw(a_r, k == 0))
                        g_t, g_r = stg.next()
                        evac(g_t[0:n, 0:256], a_t[0:n, 0:256], reads=[a_r], writes=[g_r])
                        ti = tile_index[t0]
                        hv0 = vc0 // 128
                        DMA(V_s[hv0:hv0 + 2, 0:n, ti, :].rearrange("h p c -> p h c"),
                            g_t[0:n, 0:256].rearrange("p (h c) -> p h c", h=2), reads=[g_r],
                            pwrites=[V_res], key=g_r)
                else:
                    fc0 = col0 - 2 * cfg.QK - ATT
                    g = fc0 // GD
                    cc0 = (fc0 % GD) // 128
                    for hh in range(2):
                        for (m0, mw) in mch:
                            a_t, a_r = acc.next()
                            for k in range(KD):
                                P.add("pe", i_mm(a_t[:, 0:mw], w_t[:, k, hh * 128:(hh + 1) * 128],
                                                 hnT[:, k, m0:m0 + mw], k == 0, k == KD - 1),
                                      reads=[w_r, hnT_r], **acc_rw(a_r, k == 0))
                            evac(fT[:, cc0 + hh, m0:m0 + mw], a_t[:, 0:mw], reads=[a_r], pwrites=[fT_r])
                    if cc0 + 2 >= GC:
                        for (t0, n) in stiles:
                            ti = tile_index[t0]
                            for cs in range(2):
                                a_t, a_r = acc.next()
                                for c in range(GC):
                                    P.add("pe", i_mm(a_t[0:n, 0:GD], fT[:, c, t0 - s0:t0 - s0 + n],
                                                     cdft[:, cs, c, :], c == 0, c == GC - 1),
                                          reads=[fT_r, cdft_r], **acc_rw(a_r, c == 0))
                                g_t, g_r = stg.next()
                                evac(g_t[0:n, 0:GD], a_t[0:n, 0:GD], reads=[a_r], writes=[g_r])
                                DMA(G_s[cs, g, 0:n, ti, :], g_t[0:n, 0:GD],
                                    reads=[g_r], pwrites=[G_res[g]], key=g_r)

            pipeline(len(items), pj_load, pj_compute, 2)
            P.emit_phase("proj")

        aq = split(NOWN, 256)
        AW = max(w for _, w in aq)
        NAQ = len(aq)
        with ExitStack() as st:
            posq, posq_r = sb(nc, st, "at_posq", [128, NOWN], F32)
            posk, posk_r = sb(nc, st, "at_posk", [128, NKC], F32)
            lamv, lamv_r = sb(nc, st, "at_lamv", [128, 4, 64], F32)
            lprod, lprod_r = sb(nc, st, "at_lprod", [128, 2, 64], F32)
            lsum, lsum_r = sb(nc, st, "at_lsum", [128, 2], F32)
            lexp, lexp_r = sb(nc, st, "at_lexp", [128, 2], F32)
            nlam, nlam_r = sb(nc, st, "at_nlam", [128, 1], F32)
            subw0, subw0_r = sb(nc, st, "at_subw0", [128, 1], F32)
            subw, subw_r = sb(nc, st, "at_subw", [128, 1], F32)
            onesb, onesb_r = sb(nc, st, "at_ones", [128, 128], BF16)
            Dt = [st.enter_context(nc.sbuf_tensor(f"at_D{b}", [128, NKC, AW], F32)) for b in range(2)]
            D_r = [[Res(f"D{b}_{j}") for j in range(NKC)] for b in range(2)]
            KT = Ring(nc, st, "at_KT", [128, NTOK], BF16, 3)
            Vt = Ring(nc, st, "at_V", [128, NKC, 128], BF16, 3)
            QT = Ring(nc, st, "at_QT", [128, 2 * AW], BF16, 3)
            tmp = Ring(nc, st, "at_tmp", [128, 2 * AW], F32, 3)
            Et = Ring(nc, st, "at_E", [128, 2 * AW], BF16, 4)
            LAG = 4
            Sp = Ring(nc, st, "at_S", [128, 512], F32, LAG, psum=True)
            accN = [st.enter_context(nc.psum_tensor(f"at_accN{i}", [128, 512], F32)) for i in range(2)]
            accD = [st.enter_context(nc.psum_tensor(f"at_accD{i}", [128, 512], F32)) for i in range(2)]
            accN_r = [Res(f"at_accN{i}") for i in range(2)]
            accD_r = [Res(f"at_accD{i}") for i in range(2)]
            lden = Ring(nc, st, "at_lden", [128, 2 * AW], F32, 2)
            rden = Ring(nc, st, "at_rden", [128, 2 * AW], F32, 2)
            a12 = Ring(nc, st, "at_a12", [128, 2 * AW], F32, 2)
            attn = Ring(nc, st, "at_attn", [128, AW], F32, 3)
            sq = Ring(nc, st, "at_sq", [128, AW], BF16, 3)
            rstd = Ring(nc, st, "at_rstd", [128, AW], F32, 2)
            ostg = Ring(nc, st, "at_o", [128, AW], BF16, 3)
            lnr = Ring(nc, st, "at_ln", [128, AW], F32, 2)
            epsc, epsc_r = sb(nc, st, "at_eps", [128, 1], F32)
            nposk, nposk_r = sb(nc, st, "at_nposk", [128, NKC], F32)
            P.add("dve", i_memset(epsc[:], SUBLN_EPS), writes=[epsc_r])

            DMA(posq[:], posq_d[:, :], writes=[posq_r], key=posq_r)
            DMA(posk[:], posk_d[:, :], writes=[posk_r], key=posk_r)
            P.add("dve", i_ts(nposk[:, :], posk[:, :], -1.0, None, ALU.mult), reads=[posk_r], writes=[nposk_r])
            DMA(lamv[:], lam_d[:, :, :], writes=[lamv_r], key=lamv_r)
            DMA(subw0[:], subw_d[:, :], writes=[subw0_r], key=subw0_r)
            P.add("pool", i_memset(onesb[:], 1.0), writes=[onesb_r])
            P.add("dve", i_tt(lprod[:, 0, :], lamv[:, 0, :], lamv[:, 1, :], ALU.mult), reads=[lamv_r], writes=[lprod_r])
            P.add("dve", i_tt(lprod[:, 1, :], lamv[:, 2, :], lamv[:, 3, :], ALU.mult), reads=[lamv_r], pwrites=[lprod_r])
            P.add("dve", i_reduce(lsum[:, :], lprod[:, :, :], AX.X, ALU.add), reads=[lprod_r], writes=[lsum_r])
            P.add("act", i_act(lexp[:, :], lsum[:, :], AF.Exp), reads=[lsum_r], writes=[lexp_r])
            P.add("dve", i_stt(nlam[:, :], lexp[:, 1:2], -cfg.lam_init, lexp[:, 0:1], ALU.add, ALU.subtract),
                  reads=[lexp_r], writes=[nlam_r])
            P.add("dve", i_ts(subw[:, :], subw0[:, :], (1.0 - cfg.lam_init), None, ALU.mult),
                  reads=[subw0_r], writes=[subw_r])

            inv_sqrt_dv = 1.0 / math.sqrt(128.0)

            def v2(t, n0, n1, w):
                return t[n0:n1, 0:2 * AW].rearrange("p (c a) -> p c a", c=2)[:, :, 0:w]

            items = [(qi, h) for qi in range(NAQ) for h in range(H)]
            NI = len(items)
            ld = {}

            def at_load(i):
                qi, h = items[i]
                q0, w = aq[qi]
                k_t, k_r = KT.next()
                v_t, v_r = Vt.next()
                q_t, q_r = QT.next()
                ld[i] = (k_t, k_r, v_t, v_r, q_t, q_r)
                DMA(k_t[:], KT_s[h], reads=[KT_res[h]], writes=[k_r], key=k_r)
                DMA(v_t[:], V_s[h],
                    reads=[V_res], writes=[v_r], key=v_r)
                DMA(q_t[0:64, 0:w], QT_s[h, 0:64, q0:q0 + w], reads=[QT_res[h]], writes=[q_r], key=q_r)
                DMA(q_t[64:128, AW:AW + w], QT_s[h, 64:128, q0:q0 + w], reads=[QT_res[h]], pwrites=[q_r], key=q_r)

            def gen_D(qi, js):
                q0, w = aq[qi]
                b = qi % 2
                for j in js:
                    t0, nj = cfg.tiles[j]
                    P.add("act", i_act(Dt[b][0:nj, j, 0:w], posq[0:nj, q0:q0 + w], AF.Abs, bias=nposk[0:nj, j:j + 1]),
                          reads=[posq_r, nposk_r], writes=[D_r[b][j]])

            SKIP_THR = 200.0
            MIN_TILES = 14
            tpos = cfg.tile_pos_ranges()
            need = {}
            for i_, (qi_, h_) in enumerate(items):
                q0_, w_ = aq[qi_]
                lst = []
                for j_ in range(NKC):
                    near = False
                    for par in range(2):
                        base = (cfg.NMETA - 1) + cfg.HALF * par
                        qmin, qmax = base + q0_, base + q0_ + w_ - 1
                        pmin, pmax = tpos[par][j_]
                        d = max(0, pmin - qmax, qmin - pmax)
                        if cfg.slopes[h_] * d < SKIP_THR:
                            near = True
                    if near:
                        lst.append(j_)
                if len(lst) < MIN_TILES:
                    extra = [j_ for j_ in range(NKC) if j_ not in lst][:MIN_TILES - len(lst)]
                    lst = sorted(lst + extra)
                need[i_] = lst
            G = [(i, j) for i in range(NI) for j in need[i]]
            NGI = len(G)
            sdict = {}

            def qk(g):
                i, j = G[g]
                qi, h = items[i]
                q0, w = aq[qi]
                k_t, k_r, v_t, v_r, q_t, q_r = ld[i]
                t0, nj = cfg.tiles[j]
                s_t, s_r = Sp.next()
                P.add("pe", i_mm(s_t[0:nj, 0:AW + w], k_t[:, t0:t0 + nj], q_t[:, 0:AW + w], True, True),
                      reads=[k_r, q_r], writes=[s_r])
                sdict[g] = (s_t, s_r)

            def epilogue_stages(i):
                qi, h = items[i]
                q0, w = aq[qi]
                s = i % 2
                ld_t, ld_r = lden.next()
                rd_t, rd_r = rden.next()
                a_t, a_r = a12.next()
                at_t, at_r = attn.next()
                sq_t, sq_r = sq.next()
                rs_t, rs_r = rstd.next()
                o_t, o_r = ostg.next()
                ln_t, ln_r = lnr.next()

                def s1():
                    P.add("act", i_act(v2(ld_t, 0, 128, w), v2(accD[s], 0, 128, w), AF.Ln), reads=[accD_r[s]], writes=[ld_r])
                    P.add("act", i_act(v2(rd_t, 0, 128, w), v2(ld_t, 0, 128, w), AF.Exp, scale=-1.0), reads=[ld_r], writes=[rd_r])

                def s2():
                    P.add("dve", i_tt(v2(a_t, 0, 128, w), v2(accN[s], 0, 128, w), v2(rd_t, 0, 128, w), ALU.mult),
                          reads=[accN_r[s], rd_r], writes=[a_r])
                    P.add("dve", i_stt(at_t[:, 0:w], a_t[:, AW:AW + w], nlam[:, 0:1], a_t[:, 0:w], ALU.mult, ALU.add),
                          reads=[a_r, nlam_r], writes=[at_r])

                def s3():
                    P.add("act", i_act(sq_t[:, 0:w], at_t[:, 0:w], AF.Square, scale=inv_sqrt_dv),
                          reads=[at_r], writes=[sq_r])

                def s4():
                    P.add("pe", i_mm(accD[s][:, 0:w], onesb[:, :], sq_t[:, 0:w], True, True),
                          reads=[sq_r, onesb_r], writes=[accD_r[s]])

                def s5():
                    P.add("act", i_act(ln_t[:, 0:w], accD[s][:, 0:w], AF.Ln, bias=epsc[:, 0:1]),
                          reads=[accD_r[s], epsc_r], writes=[ln_r])
                    P.add("act", i_act(rs_t[:, 0:w], ln_t[:, 0:w], AF.Exp, scale=-0.5), reads=[ln_r], writes=[rs_r])

                def s6():
                    P.add("dve", i_stt(o_t[:, 0:w], at_t[:, 0:w], subw[:, 0:1], rs_t[:, 0:w], ALU.mult, ALU.mult),
                          reads=[at_r, rs_r, subw_r], writes=[o_r])
                    DMA(mixT_s[h * 128:(h + 1) * 128, q0:q0 + w], o_t[:, 0:w], reads=[o_r], pwrites=[mixA_res[h]],
                        key=o_r)

                return [(2, s1), (4, s2), (6, s3), (8, s4), (10, s5), (12, s6)]

            for (tt_, rr_) in zip(QT.t + Et.t, QT.r + Et.r):
                P.add("pool", i_memset(tt_[:], 0.0), writes=[rr_])
                rr_.readers.append(P.all[-1])
            prep_wdn()
            assert NKC >= 14
            gen_D(0, range(NKC))
            for i in range(min(2, NI)):
                at_load(i)
            for g in range(min(LAG, NGI)):
                qk(g)
            pend = []
            per_item_D = (NKC + H - 1) // H
            for g in range(NGI):
                i, j = G[g]
                qi, h = items[i]
                q0, w = aq[qi]
                k_t, k_r, v_t, v_r, q_t, q_r = ld[i]
                t0, nj = cfg.tiles[j]
                cneg = -8.0 * cfg.slopes[h]
                b = qi % 2
                s = i % 2
                jfirst = (j == need[i][0])
                jlast = (j == need[i][-1])
                if jfirst:
                    if i + 2 < NI:
                        at_load(i + 2)
                    if qi + 1 < NAQ:
                        gen_D(qi + 1, range(h * per_item_D, min(NKC, (h + 1) * per_item_D)))
                s_t, s_r = sdict.pop(g)
                m_t, m_r = tmp.next()
                e_t, e_r = Et.next()
                dbc = Dt[b][0:nj, j, 0:w].unsqueeze(1).to_broadcast([nj, 2, w])
                P.add("dve", i_stt(v2(m_t, 0, nj, w), dbc, cneg, v2(s_t, 0, nj, w), ALU.mult, ALU.add),
                      reads=[D_r[b][j], s_r], writes=[m_r])
                P.add("act", i_act(v2(e_t, 0, nj, w), v2(m_t, 0, nj, w), AF.Exp, scale=0.125),
                      reads=[m_r], writes=[e_r])
                if g + LAG < NGI:
                    qk(g + LAG)
                P.add("pe", i_mm(accN[s][:, 0:AW + w], v_t[0:nj, j, :], e_t[0:nj, 0:AW + w], jfirst, jlast),
                      reads=[e_r, v_r], **acc_rw(accN_r[s], jfirst))
                P.add("pe", i_mm(accD[s][:, 0:AW + w], onesb[0:nj, :], e_t[0:nj, 0:AW + w], jfirst, jlast),
                      reads=[e_r, onesb_r], **acc_rw(accD_r[s], jfirst))
                while pend and pend[0][0] <= g:
                    pend.pop(0)[1]()
                if jlast:
                    for (dl, fn) in epilogue_stages(i):
                        pend.append((g + dl, fn))
                    pend.sort(key=lambda x: x[0])
                    ld.pop(i)
            while pend:
                pend.pop(0)[1]()
            P.emit_phase("att")

        JB = 4
        nrm = 1.0 / math.sqrt(cfg.L * GD)
        with ExitStack() as st:
            G1r = Ring(nc, st, "fo_G1", [128, NKC, GD], BF16, 2)
            G2r = Ring(nc, st, "fo_G2", [128, NKC, GD], BF16, 2)
            slab = Ring(nc, st, "fo_slab", [128, JB, 2, QW], BF16, 4)
            fw0, fw0_r = sb(nc, st, "fo_fw0", [128, NG * GC], F32)
            fw, fw_r = sb(nc, st, "fo_fw", [128, NG * GC], F32)
            onesb, onesb_r = sb(nc, st, "fo_ones", [128, 128], BF16)
            acc = Ring(nc, st, "fo_acc", [128, 512], F32, 8, psum=True)
            sq = Ring(nc, st, "fo_sq", [128, QW], BF16, 4)
            rstd = Ring(nc, st, "fo_rstd", [128, QW], F32, 2)
            ostg = Ring(nc, st, "fo_o", [128, QW], BF16, 4)
            lnr = Ring(nc, st, "fo_ln", [128, QW], F32, 2)
            epsc, epsc_r = sb(nc, st, "fo_eps", [128, 1], F32)
            P.add("dve", i_memset(epsc[:], NORM_EPS), writes=[epsc_r])
            DMA(fw0[:], fw_d[:, :], writes=[fw0_r], key=fw0_r)
            P.add("pool", i_memset(onesb[:], 1.0), writes=[onesb_r])
            P.add("dve", i_ts(fw[:, :], fw0[:, :], nrm, None, ALU.mult), reads=[fw0_r], writes=[fw_r])
            sq_scale = nrm / math.sqrt(GD)
            slabs = list(range(0, NKC, JB))
            items = [(g, qi, si) for g in range(NG) for qi in range(NQC) for si in range(len(slabs))]
            gl = {}
            sl = {}
            fstate = {}

            def fo_load_g(g):
                g1 = G1r.next()
                g2 = G2r.next()
                gl[g] = (g1, g2)
                for cs, (gt, gr) in enumerate((g1, g2)):
                    DMA(gt[:], G_s[cs, g],
                        reads=[G_res[g]], writes=[gr], key=gr)

            def fo_load(i):
                g, qi, si = items[i]
                j0 = slabs[si]
                jn = min(JB, NKC - j0)
                sl_t, sl_r = slab.next()
                sl[i] = (sl_t, sl_r)
                DMA(sl_t[:, 0:jn, :, :],
                    dft_d[qi, j0 * 128:(j0 + jn) * 128, :, :].rearrange("(j p) s q -> p j s q", p=128),
                    writes=[sl_r], key=sl_r)

            def fo_compute(i):
                g, qi, si = items[i]
                q0, w = cfg.qchunks[qi]
                j0 = slabs[si]
                jn = min(JB, NKC - j0)
                sl_t, sl_r = sl.pop(i)
                if qi == 0 and si == 0 and g + 1 < NG:
                    fo_load_g(g + 1)
                if si == 0:
                    fstate["a"] = [acc.next() for _ in range(GC)]
                a = fstate["a"]
                gpair = gl[g]
                for jj in range(jn):
                    j = j0 + jj
                    nj = cfg.tiles[j][1]
                    for cs in range(2):
                        gt, gr = gpair[cs]
                        first = (j == 0 and cs == 0)
                        lastm = (j == NKC - 1 and cs == 1)
                        for cc in range(GC):
                            P.add("pe", i_mm(a[cc][0][:, 0:w], gt[0:nj, j, cc * 128:(cc + 1) * 128],
                                             sl_t[0:nj, jj, cs, 0:w], first, lastm),
                                  reads=[gr, sl_r], **acc_rw(a[cc][1], first))
                if si != len(slabs) - 1:
                    return
                ss_t, ss_r = acc.next()
                sqs = []
                for cc in range(GC):
                    sq_t, sq_r = sq.next()
                    sqs.append((sq_t, sq_r))
                    P.add("act", i_act(sq_t[:, 0:w], a[cc][0][:, 0:w], AF.Square, scale=sq_scale),
                          reads=[a[cc][1]], writes=[sq_r])
                for cc in range(GC):
                    P.add("pe", i_mm(ss_t[:, 0:w], onesb[:, :], sqs[cc][0][:, 0:w], cc == 0, cc == GC - 1),
                          reads=[sqs[cc][1], onesb_r], **acc_rw(ss_r, cc == 0))
                rs_t, rs_r = rstd.next()
                ln_t, ln_r = lnr.next()
                P.add("act", i_act(ln_t[:, 0:w], ss_t[:, 0:w], AF.Ln, bias=epsc[:, 0:1]), reads=[ss_r, epsc_r],
                      writes=[ln_r])
                P.add("act", i_act(rs_t[:, 0:w], ln_t[:, 0:w], AF.Exp, scale=-0.5), reads=[ln_r], writes=[rs_r])
                for cc in range(GC):
                    o_t, o_r = ostg.next()
                    col = g * GC + cc
                    P.add("dve", i_stt(o_t[:, 0:w], a[cc][0][:, 0:w], fw[:, col:col + 1], rs_t[:, 0:w],
                                       ALU.mult, ALU.mult), reads=[a[cc][1], rs_r, fw_r], writes=[o_r])
                    r0 = ATT + g * GD + cc * 128
                    DMA(mixT_s[r0:r0 + 128, q0:q0 + w], o_t[:, 0:w], reads=[o_r], pwrites=[mixF_res[g]], key=o_r)

            fo_load_g(0)
            pipeline(len(items), fo_load, fo_compute, 3)
            P.emit_phase("four")

        own_tiles = cfg.tiles[:cfg.n_own_tiles]
        wgroups, cur = [], []
        for (t0, n) in own_tiles:
            if cur and (t0 + n - cur[0][0]) > 528:
                wgroups.append(cur)
                cur = []
            cur.append((t0, n))
        if cur:
            wgroups.append(cur)
        mix_all = mixA_res + mixF_res
        with ExitStack() as st:
            mixT = Ring(nc, st, "wo_mix", [128, KD, 528], BF16, 2)
            wblk = Ring(nc, st, "wo_w", [128, KD, 512], BF16, 2)
            xp = Ring(nc, st, "wo_x", [128, 512], F32, 12)
            hp = Ring(nc, st, "wo_h", [128, 512], F32, 4)
            acc = Ring(nc, st, "wo_acc", [128, 512], F32, 8, psum=True)
            items = [(gi, nb) for gi in range(len(wgroups)) for nb in range(NB)]
            ms = {}
            ws = {}
            xs = {}

            def wo_load_mix(gi):
                grp = wgroups[gi]
                g0 = grp[0][0]
                gn = grp[-1][0] + grp[-1][1] - g0
                m_t, m_r = mixT.next()
                ms[gi] = (m_t, m_r)
                DMA(m_t[:, :, 0:gn], kp(mixT_s[:, g0:g0 + gn]), reads=mix_all, writes=[m_r], key=m_r)

            def wo_load(i):
                gi, nb = items[i]
                w_t, w_r = wblk.next()
                ws[i] = (w_t, w_r)
                DMA(w_t[:], woutb[nb], reads=[wout_res[nb]], writes=[w_r], key=w_r)
                xl = []
                for (t0, n) in wgroups[gi]:
                    x_t, x_r = xp.next()
                    xl.append((x_t, x_r))
                    DMA(x_t[0:n, :], xall[t0:t0 + n, nb * 512:(nb + 1) * 512], writes=[x_r], key=x_r)
                xs[i] = xl

            def wo_compute(i):
                gi, nb = items[i]
                grp = wgroups[gi]
                g0 = grp[0][0]
                if nb == 0 and gi + 1 < len(wgroups):
                    wo_load_mix(gi + 1)
                m_t, m_r = ms[gi]
                w_t, w_r = ws.pop(i)
                xl = xs.pop(i)
                for ti, (t0, n) in enumerate(grp):
                    a_t, a_r = acc.next()
                    x_t, x_r = xl[ti]
                    h_t, h_r = hp.next()
                    for k in range(KD):
                        P.add("pe", i_mm(a_t[0:n, :], m_t[:, k, t0 - g0:t0 - g0 + n], w_t[:, k, :], k == 0, k == KD - 1),
                              reads=[m_r, w_r], **acc_rw(a_r, k == 0))
                    P.add("dve", i_tt(h_t[0:n, :], a_t[0:n, :], x_t[0:n, :], ALU.add), reads=[a_r, x_r], writes=[h_r])
                    DMA(h2_s[t0:t0 + n, nb * 512:(nb + 1) * 512], h_t[0:n, :], reads=[h_r], pwrites=[h2_res], key=h_r)

            wo_load_mix(0)
            pipeline(len(items), wo_load, wo_compute, 1)
            P.emit_phase("wout")

        norm_phase("n2", h2_s, own_tiles, n2w_d, NORM_EPS, "T", dstT=hn2T_s, dst_res=hn2T_res, src_res=h2_res)

        fgroups = split(HALF, 512)
        TMAX = max(w for _, w in fgroups)
        KS = 4
        with ExitStack() as st:
            hn2T, hn2T_r = sb(nc, st, "ff_hn", [128, KD, TMAX + 2], BF16)
            actT = st.enter_context(nc.sbuf_tensor("ff_act", [128, NFF, TMAX], BF16))
            act_r = Res("ff_act")
            wup = Ring(nc, st, "ff_wup", [128, KD, 256], BF16, 2)
            wdn = Ring(nc, st, "ff_wdn", [128, KS, 512], BF16, 4)
            cw, cw_r = sb(nc, st, "ff_cw", [128, NFF, 3], F32)
            cbv, cb_r = sb(nc, st, "ff_cb", [128, NFF], F32)
            gm, gm_r = sb(nc, st, "ff_gm", [128, 1], F32)
            gs = Ring(nc, st, "ff_gs", [128, TMAX + 2], F32, 2)
            gh = Ring(nc, st, "ff_gh", [128, 1], F32, 2)
            cA = Ring(nc, st, "ff_cA", [128, TMAX], F32, 2)
            cB = Ring(nc, st, "ff_cB", [128, TMAX], F32, 2)
            hp = Ring(nc, st, "ff_hp", [128, 512], F32, 6)
            op = Ring(nc, st, "ff_op", [128, 512], F32, 3)
            acc = Ring(nc, st, "ff_acc", [128, 512], F32, 8, psum=True)
            DMA(cw[:], cw_d[:, :, :], writes=[cw_r], key=cw_r)
            DMA(cbv[:], cb_d[:, :], writes=[cb_r], key=cb_r)
            DMA(gm[:], mask_d[:, :], writes=[gm_r], key=gm_r)
            for gi, (r0, T) in enumerate(fgroups):
                lastg = gi == len(fgroups) - 1
                DMA(hn2T[:, :, 0:T + 2], kp(hn2T_s[:, r0:r0 + T + 2]), reads=[hn2T_res], writes=[hn2T_r], key=hn2T_r)
                gch = split(T + 2, 512)
                ws = {}

                def up_load(j):
                    w_t, w_r = wup.next()
                    ws[j] = (w_t, w_r)
                    DMA(w_t[:], wupb[j], reads=[wup_res[j]], writes=[w_r], key=w_r)

                def up_compute(j, T=T, gch=gch, lastg=lastg):
                    w_t, w_r = ws.pop(j)
                    gacc = []
                    for (c0, cwid) in gch:
                        a_t, a_r = acc.next()
                        gacc.append((a_t, a_r, c0, cwid))
                        for k in range(KD):
                            P.add("pe", i_mm(a_t[:, 0:cwid], w_t[:, k, 0:128], hn2T[:, k, c0:c0 + cwid], k == 0, k == KD - 1),
                                  reads=[w_r, hn2T_r], **acc_rw(a_r, k == 0))
                    v_t, v_r = acc.next()
                    for k in range(KD):
                        P.add("pe", i_mm(v_t[:, 0:T], w_t[:, k, 128:256], hn2T[:, k, 1:1 + T], k == 0, k == KD - 1),
                              reads=[w_r, hn2T_r], **acc_rw(v_r, k == 0))
                    g_t, g_r = gs.next()
                    for idx, (a_t, a_r, c0, cwid) in enumerate(gacc):
                        P.add("act", i_act(g_t[:, c0:c0 + cwid], a_t[:, 0:cwid], AF.Copy), reads=[a_r],
                              **acc_rw(g_r, idx == 0))
                    A_t, A_r = cA.next()
                    B_t, B_r = cB.next()
                    P.add("dve", i_ts(A_t[:, 0:T], g_t[:, 1:T + 1], cw[:, j, 1:2], cbv[:, j:j + 1], ALU.mult, ALU.add),
                          reads=[g_r, cw_r, cb_r], writes=[A_r])
                    P.add("dve", i_stt(B_t[:, 0:T], g_t[:, 0:T], cw[:, j, 0:1], A_t[:, 0:T], ALU.mult, ALU.add),
                          reads=[g_r, cw_r, A_r], writes=[B_r])
                    if lastg:
                        gh_t, gh_r = gh.next()
                        P.add("dve", i_ts(gh_t[:, 0:1], g_t[:, T + 1:T + 2], gm[:, 0:1], None, ALU.mult),
                              reads=[g_r, gm_r], writes=[gh_r])
                        P.add("dve", i_stt(A_t[:, 0:T - 1], g_t[:, 2:T + 1], cw[:, j, 2:3], B_t[:, 0:T - 1],
                                           ALU.mult, ALU.add), reads=[g_r, cw_r, B_r], writes=[A_r])
                        P.add("dve", i_stt(A_t[:, T - 1:T], gh_t[:, 0:1], cw[:, j, 2:3], B_t[:, T - 1:T],
                                           ALU.mult, ALU.add), reads=[gh_r, cw_r, B_r], pwrites=[A_r])
                    else:
                        P.add("dve", i_stt(A_t[:, 0:T], g_t[:, 2:T + 2], cw[:, j, 2:3], B_t[:, 0:T], ALU.mult, ALU.add),
                              reads=[g_r, cw_r, B_r], writes=[A_r])
                    P.add("act", i_act(B_t[:, 0:T], A_t[:, 0:T], AF.Gelu), reads=[A_r], writes=[B_r])
                    P.add("dve", i_tt(actT[:, j, 0:T], v_t[:, 0:T], B_t[:, 0:T], ALU.mult),
                          reads=[B_r, v_r], **acc_rw(act_r, j == 0))

                pipeline(NFF, up_load, up_compute, 1)

                ttiles = [(o, min(128, T - o)) for o in range(0, T, 128)]
                fslabs = list(range(0, NFF, KS))
                ditems = [(nb, si) for nb in range(NB) for si in range(len(fslabs))]
                ds = {}
                hs = {}
                dstate = {}

                def dn_load(i, ditems=ditems, fslabs=fslabs, ttiles=ttiles, r0=r0):
                    nb, si = ditems[i]
                    f0 = fslabs[si]
                    fn = min(KS, NFF - f0)
                    d_t, d_r = wdn.next()
                    ds[i] = (d_t, d_r)
                    DMA(d_t[:, 0:fn, :], wdnb[nb, :, f0:f0 + fn, :], reads=[wdn_res[nb]], writes=[d_r], key=d_r)

                def dn_compute(i, ditems=ditems, fslabs=fslabs, ttiles=ttiles, r0=r0):
                    nb, si = ditems[i]
                    f0 = fslabs[si]
                    fn = min(KS, NFF - f0)
                    d_t, d_r = ds.pop(i)
                    if si == max(0, len(fslabs) - 5):
                        hl = []
                        for (o, n) in ttiles:
                            h_t, h_r = hp.next()
                            hl.append((h_t, h_r))
                            row = r0 + o
                            DMA(h_t[0:n, :], h2_s[row + 1:row + 1 + n, nb * 512:(nb + 1) * 512], reads=[h2_res],
                                writes=[h_r], key=h_r)
                        hs[nb] = hl
                    if si == 0:
                        dstate["acc"] = [acc.next() for _ in ttiles]
                    accs_nb = dstate["acc"]
                    for ff in range(fn):
                        f = f0 + ff
                        for ti, (o, n) in enumerate(ttiles):
                            a_t, a_r = accs_nb[ti]
                            P.add("pe", i_mm(a_t[0:n, :], actT[:, f, o:o + n], d_t[:, ff, :], f == 0, f == NFF - 1),
                                  reads=[act_r, d_r], **acc_rw(a_r, f == 0))
                    if si != len(fslabs) - 1:
                        return
                    hl = hs.pop(nb)
                    for ti, (o, n) in enumerate(ttiles):
                        a_t, a_r = accs_nb[ti]
                        h_t, h_r = hl[ti]
                        o_t, o_r = op.next()
                        row = r0 + o
                        P.add("dve", i_tt(o_t[0:n, :], a_t[0:n, :], h_t[0:n, :], ALU.add), reads=[a_r, h_r], writes=[o_r])
                        DMA(y[row:row + n, nb * 512:(nb + 1) * 512], o_t[0:n, :], reads=[o_r], pwrites=[y_res], key=o_r)

                pipeline(len(ditems), dn_load, dn_compute, 3)
            P.emit_phase("ffn")

        ytiles = [(o, min(128, HALF - o)) for o in range(0, HALF, 128)]
        norm_phase("fin", y, ytiles, nfw_d, NORM_EPS, "F", dst_res=y_res, src_res=y_res, dst_rows=y)

        fin = list(P.final_events) + [(s, s.count) for s in P.persist_sems]
        with nc.Block() as block:
            def tail(eng):
                for (sem, val) in fin:
                    eng.wait_ge(sem.h, val)
            block.sync(tail)
    return nc


def prepare_inputs(cfg, x, meta_tokens, norm1_w, w_in, lambda_q1, lambda_k1, lambda_q2, lambda_k2,
                   diff_subln_w, fourier_norm_w, w_out, norm2_w, w_up, conv_w, conv_b, w_down, final_norm_w):
    f32 = np.float32
    bf16 = ml_dtypes.bfloat16
    D, L, NOWN, NTOK, HALF, NMETA, SEQ = cfg.D, cfg.L, cfg.NOWN, cfg.NTOK, cfg.HALF, cfg.NMETA, cfg.SEQ
    NKC, QW, GD, GC, NFF = cfg.NKC, cfg.QW, cfg.GD, cfg.GC, cfg.NFF
    x = np.asarray(x, f32)
    meta = np.asarray(meta_tokens, f32)
    w_in2 = np.ascontiguousarray(np.asarray(w_in, f32)[0])
    w_out2 = np.ascontiguousarray(np.asarray(w_out, f32)[0])
    w_up2 = np.ascontiguousarray(np.asarray(w_up, f32)[0])
    w_down2 = np.ascontiguousarray(np.asarray(w_down, f32)[0])
    bc = lambda v: np.ascontiguousarray(np.broadcast_to(np.asarray(v, f32).reshape(1, -1), (128, np.asarray(v).size)))
    n1w, n2w, nfw = bc(norm1_w[0]), bc(norm2_w[0]), bc(final_norm_w)
    lamv = np.ascontiguousarray(np.broadcast_to(
        np.stack([np.asarray(v, f32)[0] for v in (lambda_q1, lambda_k1, lambda_q2, lambda_k2)])[None], (128, 4, 64)))
    subw = np.ascontiguousarray(np.asarray(diff_subln_w, f32)[0].reshape(128, 1))
    fw = np.ascontiguousarray(np.asarray(fourier_norm_w, f32)[0].reshape(cfg.NG * GC, 128).T)
    cw = np.ascontiguousarray(np.asarray(conv_w, f32)[0].reshape(3, NFF, 128).transpose(2, 1, 0))
    cb = np.ascontiguousarray(np.asarray(conv_b, f32)[0].reshape(NFF, 128).T)
    ident = np.eye(128, dtype=f32).astype(bf16)
    cc = np.arange(GD)
    ang = 2.0 * np.pi * ((cc[:, None] * cc[None, :]) % GD) / GD
    cd = np.stack([np.cos(ang), np.sin(ang)]).reshape(2, GC, 128, GD).transpose(0, 2, 1, 3)
    cdft = np.ascontiguousarray(cd).astype(bf16)

    PADPOS = 1.0e6
    per_t = {}
    for t in range(2):
        own_pos, oth_pos = _positions(cfg, t)
        assert len(oth_pos) == cfg.NOTH, (len(oth_pos), cfg.NOTH)
        allpos = np.concatenate([own_pos, oth_pos])
        assert len(allpos) == NTOK and len(np.unique(allpos)) == NTOK
        ispad = allpos >= L
        posq = own_pos.astype(f32)
        posk_flat = np.where(ispad, PADPOS, allpos).astype(f32)
        posk = np.zeros((128, NKC), f32)
        for j, (t0, n) in enumerate(cfg.tiles):
            posk[:n, j] = posk_flat[t0:t0 + n]
        dft = np.zeros((len(cfg.qchunks), NKC * 128, 2, QW), f32)
        pl = allpos.astype(np.int64)
        po = own_pos.astype(np.int64)
        for qi, (q0, w) in enumerate(cfg.qchunks):
            for j, (t0, n) in enumerate(cfg.tiles):
                prod = (pl[t0:t0 + n, None] * po[None, q0:q0 + w]) % L
                a = 2.0 * np.pi * prod / L
                valid = (~ispad[t0:t0 + n])[:, None] & (po[None, q0:q0 + w] < L)
                dft[qi, j * 128:j * 128 + n, 0, :w] = np.where(valid, np.cos(a), 0.0)
                dft[qi, j * 128:j * 128 + n, 1, :w] = np.where(valid, -np.sin(a), 0.0)
        per_t[t] = dict(idx=allpos, posq=np.ascontiguousarray(np.broadcast_to(posq[None], (128, NOWN))),
                        posk=posk, dft=dft.astype(bf16),
                        gmask=np.full((128, 1), 1.0 if t == 0 else 0.0, f32))
    in_maps = []
    for c in range(2 * cfg.B):
        b, t = c // 2, c % 2
        pt = per_t[t]
        table = np.concatenate([meta, x[b], np.zeros((1, D), f32)], axis=0)
        xall = table[np.minimum(pt["idx"], L)]
        in_maps.append(dict(
            xall=xall, posq=pt["posq"], posk=pt["posk"], dft=pt["dft"], cdft=cdft,
            w_in=w_in2, w_out=w_out2, w_up=w_up2, w_down=w_down2, n1w=n1w, n2w=n2w, nfw=nfw,
            lamv=lamv, subw=subw, fw=fw, cw=cw, cb=cb, gmask=pt["gmask"], ident=ident))
    return in_maps


def run(cfg, inputs, trace=False):
    nc = build_program(cfg)
    in_maps = prepare_inputs(cfg, **inputs)
    n = 2 * cfg.B
    res = run_bass_kernel_spmd(nc, in_maps, core_ids=list(range(n)), trace=trace)
    out = np.zeros((cfg.B, cfg.SEQ, cfg.D), np.float32)
    for c in range(n):
        b, t = c // 2, c % 2
        out[b, t * cfg.HALF:(t + 1) * cfg.HALF] = res.results[c]["y"]
    return out, res


def kernel(**inputs):
    cfg = Cfg()
    out, _ = run(cfg, inputs)
    return out
```
